# Optimizing a Trainium2 kernel written in Bass

```python
import jax, jax.numpy as jnp
from jax import lax
import numpy as np

D_MODEL = 1024
BATCH = 32
SEQ = 2048
DEPTH = 1

CHUNK = 64
CONV_WIDTH = 3
CONV_DIM = D_MODEL // 2
CONV_GROUPS = 8
RET_HEADS = 4
RET_DIM = D_MODEL - CONV_DIM
RET_HEAD_DIM = RET_DIM // RET_HEADS
MIX_DIM = CONV_DIM + RET_DIM
IN_DIM = 3 * CONV_DIM + 4 * RET_DIM
D_FF = 4 * D_MODEL
ROPE_BASE = 10000.0
DECAY_OFFSET = 5.0
EPS = 1e-6
N_MOD = 6

kernel_name = 'hybrid_shortconv_retention_sandwich_adaln'


def rms_norm(x, g):
    xf = x.astype(jnp.float32)
    y = xf * lax.rsqrt(jnp.mean(xf * xf, axis=-1, keepdims=True) + EPS)
    return (y * g.astype(jnp.float32)).astype(x.dtype)


def rotary(t, positions):
    half = t.shape[-1] // 2
    inv_freq = ROPE_BASE ** (-jnp.arange(half, dtype=jnp.float32) / half)
    ang = positions.astype(jnp.float32)[..., None] * inv_freq
    cos = jnp.cos(ang)[:, :, None, :]
    sin = jnp.sin(ang)[:, :, None, :]
    tf = t.astype(jnp.float32)
    t1, t2 = tf[..., :half], tf[..., half:]
    out = jnp.concatenate([t1 * cos - t2 * sin, t2 * cos + t1 * sin], axis=-1)
    return out.astype(t.dtype)


def short_conv_mixer(xin, b_gate, c_gate, conv_w):
    S = xin.shape[1]
    u = c_gate * xin
    up = jnp.pad(u, ((0, 0), (CONV_WIDTH - 1, 0), (0, 0)))
    y = up[:, 0:S] * conv_w[0]
    for j in range(1, CONV_WIDTH):
        y = y + up[:, j:j + S] * conv_w[j]
    return b_gate * y


def retention_mixer(q, k, v, g, positions):
    B, S, _ = q.shape
    H, dh, C = RET_HEADS, RET_HEAD_DIM, CHUNK
    NC = S // C
    dt = q.dtype
    q = rotary(q.reshape(B, S, H, dh), positions)
    k = rotary(k.reshape(B, S, H, dh), positions) * (dh ** -0.5)
    v = v.reshape(B, S, H, dh)

    def to_chunks(t):
        return t.reshape(B, NC, C, H, dh).transpose(0, 3, 1, 2, 4)

    q, k, v = to_chunks(q), to_chunks(k), to_chunks(v)

    log_gamma = jnp.log1p(-jnp.exp2(-DECAY_OFFSET - jnp.arange(H, dtype=jnp.float32)))
    idx = jnp.arange(C, dtype=jnp.float32)
    intra_dec = jnp.exp(log_gamma[:, None, None] * jnp.abs(idx[:, None] - idx[None, :]))
    q_dec = jnp.exp(log_gamma[:, None] * (idx + 1.0))
    k_dec = jnp.exp(log_gamma[:, None] * (C - 1.0 - idx))
    chunk_dec = jnp.exp(log_gamma * C)

    scores = jnp.einsum('bhncd,bhnmd->bhncm', q, k) * intra_dec[:, None].astype(dt)
    o_intra = jnp.einsum('bhncm,bhnmd->bhncd', scores, v)

    kv = jnp.einsum('bhnmd,bhnme->nbhde', k * k_dec[:, None, :, None].astype(dt), v)
    kv = kv.astype(jnp.float32)

    def step(state, kv_n):
        return state * chunk_dec[:, None, None] + kv_n, state

    _, s_prev = lax.scan(step, jnp.zeros((B, H, dh, dh), jnp.float32), kv)
    o_cross = jnp.einsum('bhncd,nbhde->bhnce',
                         q * q_dec[:, None, :, None].astype(dt), s_prev.astype(dt))

    o = (o_intra + o_cross).transpose(0, 2, 3, 1, 4).reshape(B, S, H, dh)
    of = o.astype(jnp.float32)
    of = of * lax.rsqrt(jnp.mean(of * of, axis=-1, keepdims=True) + EPS)
    o = of.astype(dt).reshape(B, S, RET_DIM)
    return jax.nn.silu(g) * o


def setup_inputs(seed: int = 0) -> dict:
    key = jax.random.key(seed)
    ks = jax.random.split(key, 16)
    f32 = jnp.float32
    x = jax.random.normal(ks[0], (BATCH, SEQ, D_MODEL), f32)
    c = jax.random.normal(ks[1], (BATCH, D_MODEL), f32)
    offset = jax.random.randint(ks[2], (BATCH, 1), 0, 4096, dtype=jnp.int32)
    positions = offset + jnp.arange(SEQ, dtype=jnp.int32)[None, :]
    w_ada = jax.random.normal(ks[3], (DEPTH, D_MODEL, N_MOD * D_MODEL), f32) * (0.5 * D_MODEL ** -0.5)
    b_ada = jax.random.normal(ks[4], (DEPTH, N_MOD * D_MODEL), f32) * 0.01
    g_pre_mix = 1.0 + 0.05 * jax.random.normal(ks[5], (DEPTH, D_MODEL), f32)
    g_post_mix = 1.0 + 0.05 * jax.random.normal(ks[6], (DEPTH, D_MODEL), f32)
    w_in = jax.random.normal(ks[7], (DEPTH, D_MODEL, IN_DIM), f32) * D_MODEL ** -0.5
    conv_w = jax.random.normal(ks[8], (DEPTH, CONV_WIDTH, CONV_DIM), f32) * CONV_WIDTH ** -0.5
    w_out = jax.random.normal(ks[9], (DEPTH, MIX_DIM, D_MODEL), f32) * MIX_DIM ** -0.5
    g_pre_mlp = 1.0 + 0.05 * jax.random.normal(ks[10], (DEPTH, D_MODEL), f32)
    g_post_mlp = 1.0 + 0.05 * jax.random.normal(ks[11], (DEPTH, D_MODEL), f32)
    w_fc1 = jax.random.normal(ks[12], (DEPTH, D_MODEL, D_FF), f32) * D_MODEL ** -0.5
    w_fc2 = jax.random.normal(ks[13], (DEPTH, D_FF, D_MODEL), f32) * D_FF ** -0.5
    return {'x': x, 'c': c, 'positions': positions, 'w_ada': w_ada, 'b_ada': b_ada,
            'g_pre_mix': g_pre_mix, 'g_post_mix': g_post_mix, 'w_in': w_in,
            'conv_w': conv_w, 'w_out': w_out, 'g_pre_mlp': g_pre_mlp,
            'g_post_mlp': g_post_mlp, 'w_fc1': w_fc1, 'w_fc2': w_fc2}


def reference(x, c, positions, w_ada, b_ada, g_pre_mix, g_post_mix, w_in, conv_w,
              w_out, g_pre_mlp, g_post_mlp, w_fc1, w_fc2):
    split_at = [CONV_DIM, 2 * CONV_DIM, 3 * CONV_DIM,
                3 * CONV_DIM + RET_DIM, 3 * CONV_DIM + 2 * RET_DIM, 3 * CONV_DIM + 3 * RET_DIM]
    for layer in range(DEPTH):
        mod = jax.nn.silu(c) @ w_ada[layer] + b_ada[layer]
        shift1, scale1, gate1, shift2, scale2, gate2 = [
            m[:, None, :] for m in jnp.split(mod, N_MOD, axis=-1)]

        h = rms_norm(x, g_pre_mix[layer]) * (1.0 + scale1) + shift1
        proj = h @ w_in[layer]
        xin, b_gate, c_gate, q, k, v, g = jnp.split(proj, split_at, axis=-1)
        y_conv = short_conv_mixer(xin, b_gate, c_gate, conv_w[layer])
        y_ret = retention_mixer(q, k, v, g, positions)
        mix = jnp.concatenate([y_conv, y_ret], axis=-1) @ w_out[layer]
        x = x + gate1 * rms_norm(mix, g_post_mix[layer])

        h = rms_norm(x, g_pre_mlp[layer]) * (1.0 + scale2) + shift2
        f = jnp.square(jax.nn.relu(h @ w_fc1[layer])) @ w_fc2[layer]
        x = x + gate2 * rms_norm(f, g_post_mlp[layer])
    return x
```

```python
import os
import numpy as np
from contextlib import ExitStack
import concourse.bass as bass
import concourse.mybir as mybir
from concourse.bass_utils import run_bass_kernel_spmd

F32 = mybir.dt.float32
BF16 = mybir.dt.bfloat16
I32 = mybir.dt.int32
ALU = mybir.AluOpType
AF = mybir.ActivationFunctionType
AX = mybir.AxisListType
PI = float(np.pi)

D = 1024
SEQ = 2048
NB_CORE = 4
DFF = 4096
EPS = 1e-6
NWSLOT = 5
XSLOTS = 8
GAMMA = [1.0 - 2.0 ** (-5 - h) for h in range(4)]


class _Op:
    __slots__ = ("eng", "fn", "deps", "raw", "is_dma", "semkey", "val", "sig", "idx")


class Sched:
    ENG = ("pe", "act", "dve", "pool", "sp")

    def __init__(self):
        self.ops = []
        self.last_w = {}
        self.readers = {}
        self.dma_cnt = {}

    def _add(self, eng, fn, r, w, is_dma, semkey):
        op = _Op()
        op.eng, op.fn, op.is_dma, op.semkey = eng, fn, is_dma, semkey
        op.idx = len(self.ops)
        op.sig = is_dma
        deps = {}
        for b in r:
            lw = self.last_w.get(b)
            if lw is not None:
                deps[lw] = True
        for b in w:
            lw = self.last_w.get(b)
            if lw is not None:
                deps.setdefault(lw, b in r)
            last_rd = {}
            for rd in self.readers.get(b, ()):
                if rd == op.idx:
                    continue
                rop = self.ops[rd]
                if rop.is_dma:
                    deps.setdefault(rd, False)
                else:
                    last_rd[rop.eng] = max(last_rd.get(rop.eng, -1), rd)
            for rd in last_rd.values():
                deps.setdefault(rd, False)
        op.deps = deps
        for b in r:
            if b not in w:
                self.readers.setdefault(b, []).append(op.idx)
        for b in w:
            self.last_w[b] = op.idx
            self.readers[b] = []
        if is_dma:
            self.dma_cnt[semkey] = self.dma_cnt.get(semkey, 0) + 16
            op.val = self.dma_cnt[semkey]
        self.ops.append(op)
        return op

    def op(self, eng, fn, r=(), w=()):
        return self._add(eng, fn, tuple(r), tuple(w), False, None)

    def dma(self, eng, out, in_, r=(), w=(), key=None):
        return self._add(eng, (lambda e, o=out, i=in_: e.dma_start(out=o, in_=i)), tuple(r), tuple(w), True, key)

    def dma_t(self, eng, out, in_, r=(), w=(), key=None):
        return self._add(eng, (lambda e, o=out, i=in_: e.dma_start_transpose(out=o, in_=i)), tuple(r), tuple(w), True, key)

    def emit(self, nc, es):
        ops = self.ops
        need = []
        for op in ops:
            lst = []
            for d, raw in op.deps.items():
                dop = ops[d]
                if dop.is_dma:
                    lst.append(d)
                elif dop.eng == op.eng:
                    if raw and op.eng != "pe" and not op.is_dma:
                        lst.append(d)
                    elif op.is_dma:
                        lst.append(d)
                else:
                    lst.append(d)
            need.append(lst)
            for d in lst:
                ops[d].sig = True
        cnt = {e: 0 for e in self.ENG}
        for op in ops:
            if not op.is_dma and op.sig:
                cnt[op.eng] += 1
                op.val = cnt[op.eng]
        sems = {}
        for e in self.ENG:
            sems[e] = es.enter_context(nc.semaphore("sem_" + e))
        for k in self.dma_cnt:
            sems[("dma", k)] = es.enter_context(nc.semaphore("dsem_%d" % len(sems)))
        block = es.enter_context(nc.Block())
        per_eng = {e: [op for op in ops if op.eng == e] for e in self.ENG}
        final_dma = dict(self.dma_cnt)

        def run(e, eng):
            waited = {}
            for op in per_eng[e]:
                for d in need[op.idx]:
                    dop = ops[d]
                    sk = ("dma", dop.semkey) if dop.is_dma else dop.eng
                    if waited.get(sk, 0) >= dop.val:
                        continue
                    eng.wait_ge(sems[sk], dop.val)
                    waited[sk] = dop.val
                ins = op.fn(eng)
                if op.sig:
                    if op.is_dma:
                        ins.then_inc(sems[("dma", op.semkey)], 16)
                    else:
                        ins.then_inc(sems[e], 1)
            if e == "sp":
                for k, v in final_dma.items():
                    if isinstance(k, tuple) and k[0] == "out":
                        eng.wait_ge(sems[("dma", k)], v)

        @block.tensor
        def _(eng):
            run("pe", eng)

        @block.scalar
        def _(eng):
            run("act", eng)

        @block.vector
        def _(eng):
            run("dve", eng)

        @block.gpsimd
        def _(eng):
            run("pool", eng)

        @block.sync
        def _(eng):
            run("sp", eng)


def build_program(NT=16):
    nc = bass.Bass("TRN2", target_bir_lowering=False)
    NTOK = NT * 512

    def din(name, shape, dt=F32):
        return nc.dram_tensor(name, shape, dt, kind="ExternalInput").ap()

    x_d = din("x", [NTOK, D])
    c_d = din("c", [32, 128])
    pos_d = din("pos", [64, 128], I32)
    wada_d = din("w_ada", [D, 6 * D])
    bada_d = din("b_ada", [48, 128])
    gpm_d = din("g_pre_mix", [8, 128])
    gqm_d = din("g_post_mix", [8, 128])
    win_d = din("w_in", [D, 3584])
    cw_d = din("conv_w", [12, 128])
    wout_d = din("w_out", [D, D])
    gpl_d = din("g_pre_mlp", [8, 128])
    gql_d = din("g_post_mlp", [8, 128])
    w1_d = din("w_fc1", [D, DFF])
    w2_d = din("w_fc2", [DFF, D])
    ident_d = din("ident", [128, 128])
    invf_d = din("invf", [128, 64])
    dmask_d = din("dmask", [128, 512])
    vdec_d = din("vdec", [128, 4])
    epsr_d = din("epsr", [128, 4])
    out_d = nc.dram_tensor("out", [NTOK, D], F32, kind="ExternalOutput").ap()
    winb = nc.dram_tensor("winb", [D, 3584], BF16).ap()
    woutb = nc.dram_tensor("woutb", [D, D], BF16).ap()
    w1b = nc.dram_tensor("w1b", [D, DFF], BF16).ap()
    w2b = nc.dram_tensor("w2b", [DFF, D], BF16).ap()

    S = Sched()
    with ExitStack() as es:
        def sb(name, shape, dt):
            return es.enter_context(nc.sbuf_tensor(name, shape, dt))

        XS = [sb("xs%d" % i, [128, D], F32) for i in range(XSLOTS)]
        XN = [sb("xn%d" % i, [128, D], BF16) for i in range(4)]
        JUNK = sb("junk", [128, D], BF16)
        HT = sb("hT", [128, 8 * 512], BF16)
        XY = sb("xy", [128, 4 * 512], F32)
        U = sb("u", [128, 4 * 514], F32)
        QR = sb("qr", [128, 4 * 512], BF16)
        KR = sb("kr", [128, 4 * 512], BF16)
        VD = sb("vd", [128, 4 * 512], BF16)
        SG = sb("sg", [128, 4 * 512], BF16)
        QKT = [sb("qkt%d" % i, [128, 1024], BF16) for i in range(2)]
        ST = [sb("st%d" % i, [128, 512], BF16) for i in range(2)]
        STATE = sb("state", [128, 512], F32)
        SBF = [sb("sbf%d" % i, [128, 512], BF16) for i in range(2)]
        YR = [sb("yr%d" % i, [128, 512], BF16) for i in range(2)]
        YT = sb("yT", [128, 8 * 512], BF16)
        SCR = sb("scr", [128, 4 * 512], F32)
        FT = sb("fT", [128, 32 * 512], BF16)
        GG = [sb("gg%d" % i, [128, D], F32) for i in range(2)]
        CS = [sb("cs0", [128, 3 * 256], F32)] * 2
        TT0 = sb("tt0", [128, 256], F32)
        TT1 = sb("tt1", [128, 256], F32)
        TTI = sb("tti", [128, 256], I32)
        TT3 = sb("tt3", [128, 256], F32)
        US = sb("us", [128, 512], F32)
        DMASK = sb("dmask_s", [128, 512], F32)
        VDEC = sb("vdec_s", [128, 4], F32)
        EPSR = sb("epsr_s", [128, 4], F32)
        INVF = sb("invf_s", [128, 64], F32)
        IDF = sb("idf", [128, 128], F32)
        IDB = sb("idb", [128, 128], BF16)
        VROWS = sb("vrows", [128, 128], F32)
        VT = sb("vt", [128, 124], F32)
        SCT = sb("sct", [128, 32], F32)
        MODT = sb("modt", [128, 48 * 4], F32)
        GST = [sb("gst%d" % i, [128, 32], F32) for i in range(2)]
        GGT = [sb("ggt%d" % i, [128, 32], F32) for i in range(2)]
        POSI = sb("posi", [64, 128], I32)
        POSF = sb("posf", [64, 128], F32)
        POST = sb("post", [128, 64], F32)
        NHALF = sb("nhalf", [128, 4], F32)
        STAT = [sb("stat%d" % i, [128, 8], F32) for i in range(8)]
        WS = [sb("ws%d" % i, [128, 4096], BF16) for i in range(NWSLOT)]
        PS = es.enter_context(nc.psum_tensor("PS", [128, 4096], F32))
        PSB = PS[:].bitcast(BF16)

        def psf(bank, n=512):
            return PS[:, bank * 512: bank * 512 + n]

        def psk(*banks):
            return [("ps", b) for b in banks]

        stat_ctr = [0]

        def new_stat():
            i = stat_ctr[0] % 8
            stat_ctr[0] += 1
            return STAT[i], ("stat", i)

        SLOTCAST = os.environ.get("K_SLOTCAST", "1") == "1"
        KPRO = int(os.environ.get("K_PRO", "3"))
        if SLOTCAST:
            KPRO &= ~1
        for r in range(8 if KPRO & 1 else 0):
            S.dma("pool", winb[r * 128:(r + 1) * 128, :], win_d[r * 128:(r + 1) * 128, :], w=[("wb", "win")] if r == 7 else [], key="cast_win")
        for r in range(2 if KPRO & 1 else 0):
            S.dma("pool", woutb[r * 512:(r + 1) * 512, :], wout_d[r * 512:(r + 1) * 512, :], w=[("wb", "wout")] if r == 1 else [], key="cast_wout")
        loads = [
            (VROWS[0:32, :], c_d), (VROWS[32:80, :], bada_d), (VROWS[80:88, :], gpm_d), (VROWS[88:96, :], gqm_d),
            (VROWS[96:104, :], gpl_d), (VROWS[104:112, :], gql_d), (VROWS[112:124, :], cw_d),
            (POSI[:], pos_d), (IDF[:], ident_d), (INVF[:], invf_d), (DMASK[:], dmask_d), (VDEC[:], vdec_d), (EPSR[:], epsr_d),
        ]
        allk = ["vrows", "posi", "idf", "invf", "dmask", "vdec", "epsr"]
        for i, (o, s) in enumerate(loads):
            last = i == len(loads) - 1
            S.dma("sp", o, s, w=allk if last else [], key="pro")
        S.op("pool", lambda e: e.memset(NHALF[:], -0.5), w=["nhalf"])
        S.op("dve", lambda e: e.tensor_copy(out=IDB[:], in_=IDF[:]), r=["idf"], w=["idb"])
        S.op("pe", lambda e: e.transpose(out=psf(1, 124), in_=VROWS[0:124, :], identity=IDF[0:124, 0:124]), r=["vrows", "idf"], w=psk(1))
        S.op("dve", lambda e: e.tensor_copy(out=VT[:], in_=psf(1, 124)), r=psk(1), w=["vt"])
        S.op("act", lambda e: e.activation(out=SCT[:], in_=VT[:, 0:32], func=AF.Silu), r=["vt"], w=["sct"])
        S.op("dve", lambda e: e.tensor_copy(out=POSF[:], in_=POSI[:]), r=["posi"], w=["posf"])
        S.op("pe", lambda e: e.transpose(out=psf(2, 64), in_=POSF[:], identity=IDF[0:64, 0:64]), r=["posf", "idf"], w=psk(2))
        S.op("dve", lambda e: e.tensor_copy(out=POST[:], in_=psf(2, 64)), r=psk(2), w=["post"])
        FTF = FT[:].bitcast(F32)
        STG = [FTF[:, i * 4096:(i + 1) * 4096].rearrange("p (k n) -> p k n", k=8) for i in range(2)]
        stgk = [[("fT", oc) for oc in range(i * 16, i * 16 + 16)] for i in range(2)]
        wada_v = wada_d.rearrange("(k p) n -> p k n", p=128)
        sct_v = SCT[:].rearrange("p (b k) -> p k b", k=8)
        modt3 = MODT[:].rearrange("p (c b) -> p c b", b=4)
        def bc8(lo):
            return VT[:, lo:lo + 8].unsqueeze(2).broadcast_to([128, 8, 4])

        def g3(t):
            return t[:].rearrange("p (k b) -> p k b", b=4)

        def adaln_stage(cg_lo, cg_hi):
            for cg in range(cg_lo, cg_hi):
                si = cg % 2
                S.dma("sp", STG[si], wada_v[:, :, cg * 512:(cg + 1) * 512], w=stgk[si], key=("stg", si))
                bkm = 2 + (cg % 2)
                for k in range(8):
                    S.op("pe", (lambda e, k=k, si=si, bkm=bkm: e.matmul(PS[0:4, bkm * 512:(bkm + 1) * 512], lhsT=sct_v[:, k, :], rhs=STG[si][:, k, :],
                                                                          start=(k == 0), stop=(k == 7))), r=stgk[si] + ["sct"], w=psk(bkm))
                stq = SCR[0:4, (cg % 2) * 512:(cg % 2 + 1) * 512]
                S.op("dve", lambda e, stq=stq, bkm=bkm: e.tensor_copy(out=stq, in_=PS[0:4, bkm * 512:(bkm + 1) * 512]), r=psk(bkm), w=[("scr", cg % 2)])
                for ci in range(4):
                    cc = cg * 4 + ci
                    S.op("pe", (lambda e, cc=cc, ci=ci, stq=stq: e.transpose(out=PS[:, cc * 4:(cc + 1) * 4], in_=stq[:, ci * 128:(ci + 1) * 128], identity=IDF[0:4, 0:4])),
                         r=[("scr", cg % 2), "idf"], w=psk(0))
            if cg_hi < 12:
                return
            S.op("dve", lambda e: e.tensor_tensor(out=modt3, in0=PS[:, 0:192].rearrange("p (c b) -> p c b", b=4),
                                                  in1=VT[:, 32:80].unsqueeze(2).broadcast_to([128, 48, 4]), op=ALU.add),
                 r=psk(0) + ["vt"], w=["modt"])

            S.op("dve", lambda e: e.scalar_tensor_tensor(out=g3(GST[0]), in0=modt3[:, 8:16, :], scalar=1.0, in1=bc8(80), op0=ALU.add, op1=ALU.mult), r=["modt", "vt"], w=["gst0"])
            S.op("dve", lambda e: e.scalar_tensor_tensor(out=g3(GST[1]), in0=modt3[:, 32:40, :], scalar=1.0, in1=bc8(96), op0=ALU.add, op1=ALU.mult), r=["modt", "vt"], w=["gst1"])
            S.op("dve", lambda e: e.tensor_tensor(out=g3(GGT[0]), in0=modt3[:, 16:24, :], in1=bc8(88), op=ALU.mult), r=["modt", "vt"], w=["ggt0"])
            S.op("dve", lambda e: e.tensor_tensor(out=g3(GGT[1]), in0=modt3[:, 40:48, :], in1=bc8(104), op=ALU.mult), r=["modt", "vt"], w=["ggt1"])
        SHT = [modt3[:, 0:8, :], modt3[:, 24:32, :]]

        wg_ctr = [0]
        win_v = winb.rearrange("(k p) n -> p k n", p=128)
        wout_v = woutb.rearrange("(k p) n -> p k n", p=128)
        w1_v = w1b.rearrange("(k p) n -> p k n", p=128)
        w2_v = w2b.rearrange("(k p) n -> p k n", p=128)

        win_f = win_d.rearrange("(k p) n -> p k n", p=128)
        wout_f = wout_d.rearrange("(k p) n -> p k n", p=128)
        w1_f = w1_d.rearrange("(k p) n -> p k n", p=128)
        w2_f = w2_d.rearrange("(k p) n -> p k n", p=128)
        wl_ctr = [0]

        def wload(src, kdim, wbkey, srcf=None):
            n = wg_ctr[0]
            s = n % NWSLOT
            wg_ctr[0] += 1
            gi = n % 25
            dst = WS[s][:].rearrange("p (k n) -> p k n", k=kdim)
            if SLOTCAST:
                if n < 25:
                    S.dma("pool", dst, srcf, w=[("w", s)], key=("w", s))
                    S.dma("sp", src, dst, r=[("w", s)], w=[("wbg", gi)], key=("wst", gi))
                else:
                    S.dma("sp", dst, src, r=[("wbg", gi)], w=[("w", s)], key=("w", s))
            else:
                S.dma("sp", dst, src, r=[("wb", wbkey)], w=[("w", s)], key=("w", s))
            return dst, ("w", s)

        def tile_weight_plan():
            plan = []
            for g in (0, 2, 1, 3, 4, 5, 6):
                plan.append((win_v[:, :, g * 512:(g + 1) * 512], 8, "win", win_f[:, :, g * 512:(g + 1) * 512]))
            plan.append((wout_v[:, 0:4, :], 4, "wout", wout_f[:, 0:4, :]))
            plan.append((wout_v[:, 4:8, :], 4, "wout", wout_f[:, 4:8, :]))
            for g in range(8):
                plan.append((w1_v[:, :, g * 512:(g + 1) * 512], 8, "w1", w1_f[:, :, g * 512:(g + 1) * 512]))
            for g in range(8):
                plan.append((w2_v[:, g * 4:(g + 1) * 4, :], 4, "w2", w2_f[:, g * 4:(g + 1) * 4, :]))
            return plan

        pending = []

        def request_weights(t):
            pending.extend(tile_weight_plan())

        issued = []

        def next_weight():
            while pending and len(issued) < NWSLOT - 1:
                src, kdim, key, srcf = pending.pop(0)
                issued.append(wload(src, kdim, key, srcf))
            return issued.pop(0)

        x_rows = lambda t, j: slice((t * 4 + j) * 128, (t * 4 + j + 1) * 128)

        def xslot(t, j):
            return (t * 4 + j) % XSLOTS

        def xk(s):
            return [("x", s, 0), ("x", s, 1)]

        def load_x(t):
            for j in range(4):
                s = xslot(t, j)
                S.dma("sp", XS[s][:], x_d[x_rows(t, j), :], w=xk(s), key=("xld", s))

        def rstd_ops(ms_ap, ms_key, eps_ap=None):
            st, k = new_stat()
            n = ms_ap.shape[1]
            if eps_ap is None:
                S.op("pool", lambda e: e.tensor_scalar(out=st[:, 0:n], in0=ms_ap, scalar1=EPS, scalar2=None, op0=ALU.add),
                     r=[ms_key], w=[k])
            else:
                S.op("pool", lambda e: e.tensor_tensor(out=st[:, 0:n], in0=ms_ap, in1=eps_ap, op=ALU.add),
                     r=[ms_key, "epsr"], w=[k])
            S.op("pool", lambda e: e.tensor_tensor(out=st[:, 4:4 + n], in0=st[:, 0:n], in1=NHALF[:, 0:n], op=ALU.pow), r=[k, "nhalf"], w=[k])
            return st[:, 4:4 + n], k

        htv = HT[:].rearrange("p (k n) -> p k n", k=8)
        ytv = YT[:].rearrange("p (k n) -> p k n", k=8)
        ftv = FT[:].rearrange("p (k n) -> p k n", k=32)
        xyv = XY[:].rearrange("p (c n) -> p c n", c=4)
        uv = U[:].rearrange("p (c n) -> p c n", c=4)
        hk = lambda kc: [("hT", kc, jj) for jj in range(4)]
        bank_ctr = [0]

        def seq_setup(t):
            b = t // 4
            for i in range(2):
                base = 4 + 2 * i
                gt = g3(GGT[i])
                for kc in range(8):
                    S.op("pe", lambda e, kc=kc, base=base, gt=gt: e.matmul(PS[:, base * 512 + kc * 128: base * 512 + (kc + 1) * 128],
                                                                          lhsT=gt[:, kc, b:b + 1].broadcast_to([128, 128]), rhs=IDF[:], start=True, stop=True),
                         r=["ggt%d" % i, "idf"], w=psk(base, base + 1))
                S.op("act" if i == 0 else "dve",
                     (lambda e, base=base, i=i: e.activation(out=GG[i][:], in_=PS[:, base * 512: base * 512 + 1024], func=AF.Copy)) if i == 0 else
                     (lambda e, base=base, i=i: e.tensor_copy(out=GG[i][:], in_=PS[:, base * 512: base * 512 + 1024])),
                     r=psk(base, base + 1), w=[("gg", i)])

        def rot_tables(t):
            b = t // 4
            j0 = (t % 4) * 4
            cs = CS[0]
            csk = ("cs", 0)
            C1 = 6.28125
            C2 = 2 * PI - C1
            MW = SCR[:, 1536:2048]
            v3 = lambda tl: tl[:].rearrange("p (j f) -> p j f", j=4)
            S.op("dve", lambda e: e.tensor_tensor(out=v3(TT0), in0=POST[:, b * 16 + j0: b * 16 + j0 + 4].unsqueeze(2).broadcast_to([128, 4, 64]),
                                                  in1=INVF[:].unsqueeze(1).broadcast_to([128, 4, 64]), op=ALU.mult), r=["post", "invf"], w=["tt0"])
            S.op("dve", lambda e: e.tensor_scalar(out=TT1[:], in0=TT0[:], scalar1=1.0 / (2 * PI), scalar2=None, op0=ALU.mult), r=["tt0"], w=["tt1"])
            S.op("dve", lambda e: e.tensor_copy(out=TTI[:], in_=TT1[:]), r=["tt1"], w=["tti"])
            S.op("dve", lambda e: e.tensor_copy(out=TT3[:], in_=TTI[:]), r=["tti"], w=["tt3"])
            S.op("dve", lambda e: e.scalar_tensor_tensor(out=TT1[:], in0=TT3[:], scalar=-C1, in1=TT0[:], op0=ALU.mult, op1=ALU.add), r=["tt3", "tt0"], w=["tt1"])
            S.op("dve", lambda e: e.scalar_tensor_tensor(out=TT0[:], in0=TT3[:], scalar=-C2, in1=TT1[:], op0=ALU.mult, op1=ALU.add), r=["tt3", "tt1"], w=["tt0"])
            S.op("dve", lambda e: e.tensor_scalar(out=US[:, 0:256], in0=TT0[:], scalar1=0.5 * PI, scalar2=None, op0=ALU.add), r=["tt0"], w=["us0"])
            S.op("dve", lambda e: e.tensor_copy(out=US[:, 256:512], in_=TT0[:]), r=["tt0"], w=["us1"])
            S.op("dve", lambda e: e.tensor_scalar(out=MW, in0=US[:], scalar1=PI, scalar2=-2 * PI, op0=ALU.is_gt, op1=ALU.mult), r=["us0", "us1"], w=[("scr", 3)])
            S.op("dve", lambda e: e.tensor_tensor(out=US[:], in0=US[:], in1=MW, op=ALU.add), r=["us0", "us1", ("scr", 3)], w=["us0", "us1"])
            S.op("dve", lambda e: e.tensor_scalar(out=MW, in0=US[:], scalar1=-PI, scalar2=2 * PI, op0=ALU.is_lt, op1=ALU.mult), r=["us0", "us1"], w=[("scr", 3)])
            S.op("dve", lambda e: e.tensor_tensor(out=US[:], in0=US[:], in1=MW, op=ALU.add), r=["us0", "us1", ("scr", 3)], w=["us0", "us1"])
            S.op("act", lambda e: e.activation(out=cs[:, 0:256], in_=US[:, 0:256], func=AF.Sin), r=["us0"], w=[csk])
            S.op("act", lambda e: e.activation(out=cs[:, 256:512], in_=US[:, 256:512], func=AF.Sin, scale=-1.0), r=["us1"], w=[csk])
            S.op("act", lambda e: e.activation(out=cs[:, 512:768], in_=US[:, 256:512], func=AF.Sin), r=["us1"], w=[csk])

        def conv_stage(t):
            last_in_seq = (t % 4 == 3)
            bank_ctr[0] = 0
            wv, wk = next_weight()
            for c in range(4):
                bk = bank_ctr[0] % 4; bank_ctr[0] += 1
                for kc in range(8):
                    S.op("pe", lambda e, c=c, kc=kc, bk=bk, wv=wv: e.matmul(psf(bk), lhsT=wv[:, kc, c * 128:(c + 1) * 128], rhs=htv[:, kc, :], start=(kc == 0), stop=(kc == 7)),
                         r=[wk] + hk(kc), w=psk(bk))
                S.op("act", lambda e, c=c, bk=bk: e.activation(out=xyv[:, c, :], in_=psf(bk), func=AF.Copy), r=psk(bk), w=[("xy", c)])
            wv, wk = next_weight()
            for c in range(4):
                bk = bank_ctr[0] % 4; bank_ctr[0] += 1
                for kc in range(8):
                    S.op("pe", lambda e, c=c, kc=kc, bk=bk, wv=wv: e.matmul(psf(bk), lhsT=wv[:, kc, c * 128:(c + 1) * 128], rhs=htv[:, kc, :], start=(kc == 0), stop=(kc == 7)),
                         r=[wk] + hk(kc), w=psk(bk))
                S.op("dve", lambda e, c=c, bk=bk: e.tensor_tensor(out=uv[:, c, 2:514], in0=psf(bk), in1=xyv[:, c, :], op=ALU.mult), r=psk(bk) + [("xy", c)], w=[("u", c)])
                w0 = VT[:, 112 + c: 113 + c]
                w1 = VT[:, 116 + c: 117 + c]
                w2 = VT[:, 120 + c: 121 + c]
                S.op("dve", lambda e, c=c, w2=w2: e.tensor_scalar(out=xyv[:, c, :], in0=uv[:, c, 2:514], scalar1=w2, scalar2=None, op0=ALU.mult), r=[("u", c), "vt"], w=[("xy", c)])
                S.op("dve", lambda e, c=c, w1=w1: e.scalar_tensor_tensor(out=xyv[:, c, :], in0=uv[:, c, 1:513], scalar=w1, in1=xyv[:, c, :], op0=ALU.mult, op1=ALU.add), r=[("u", c), "vt", ("xy", c)], w=[("xy", c)])
                S.op("dve", lambda e, c=c, w0=w0: e.scalar_tensor_tensor(out=xyv[:, c, :], in0=uv[:, c, 0:512], scalar=w0, in1=xyv[:, c, :], op0=ALU.mult, op1=ALU.add), r=[("u", c), "vt", ("xy", c)], w=[("xy", c)])
                if not last_in_seq:
                    S.op("pool", lambda e, c=c: e.tensor_copy(out=uv[:, c, 0:2], in_=uv[:, c, 512:514]), r=[("u", c)], w=[("u", c)])
            wv, wk = next_weight()
            for c in range(4):
                bk = bank_ctr[0] % 4; bank_ctr[0] += 1
                for kc in range(8):
                    S.op("pe", lambda e, c=c, kc=kc, bk=bk, wv=wv: e.matmul(psf(bk), lhsT=wv[:, kc, c * 128:(c + 1) * 128], rhs=htv[:, kc, :], start=(kc == 0), stop=(kc == 7)),
                         r=[wk] + hk(kc), w=psk(bk))
                S.op("dve", lambda e, c=c, bk=bk: e.tensor_tensor(out=ytv[:, c, :], in0=psf(bk), in1=xyv[:, c, :], op=ALU.mult), r=psk(bk) + [("xy", c)], w=[("yTc", c)])

        def qkv_stage(t):
            cs = CS[0]
            csk = ("cs", 0)
            cs4 = cs[:].rearrange("p (a j f) -> p a j f", a=3, j=4)
            for gi in range(2):
                wv, wk = next_weight()
                for j in range(4):
                    bk = bank_ctr[0] % 8; bank_ctr[0] += 1
                    for kc in range(8):
                        S.op("pe", lambda e, j=j, kc=kc, bk=bk, wv=wv: e.matmul(psf(bk), lhsT=htv[:, kc, j * 128:(j + 1) * 128], rhs=wv[:, kc, :], start=(kc == 0), stop=(kc == 7)),
                             r=[wk, ("hT", kc, j)], w=psk(bk))
                    p4 = psf(bk).rearrange("p (h t f) -> p h t f", h=4, t=2)
                    if gi in (0, 1):
                        dst = (QR if gi == 0 else KR)[:, j * 512:(j + 1) * 512]
                        dkey = ("qr" if gi == 0 else "kr", j)
                        ab = ((gi * 4 + j) % 2) * 2
                        A = SCR[:, ab * 512:(ab + 1) * 512]
                        B = SCR[:, (ab + 1) * 512:(ab + 2) * 512]
                        A4 = A.rearrange("p (h t f) -> p h t f", h=4, t=2)
                        B4 = B.rearrange("p (h t f) -> p h t f", h=4, t=2)
                        cosb = cs4[:, 0, j, :].unsqueeze(1).unsqueeze(1).broadcast_to([128, 4, 2, 64])
                        nsinb = cs4[:, 1, j, :].unsqueeze(1).broadcast_to([128, 4, 64])
                        sinb = cs4[:, 2, j, :].unsqueeze(1).broadcast_to([128, 4, 64])
                        S.op("dve", lambda e, p4=p4, A4=A4, cosb=cosb: e.tensor_tensor(out=A4, in0=p4, in1=cosb, op=ALU.mult), r=psk(bk) + [csk], w=[("scr", ab)])
                        S.op("dve", lambda e, p4=p4, B4=B4, nsinb=nsinb: e.tensor_tensor(out=B4[:, :, 0, :], in0=p4[:, :, 1, :], in1=nsinb, op=ALU.mult), r=psk(bk) + [csk], w=[("scr", ab + 1)])
                        S.op("dve", lambda e, p4=p4, B4=B4, sinb=sinb: e.tensor_tensor(out=B4[:, :, 1, :], in0=p4[:, :, 0, :], in1=sinb, op=ALU.mult), r=psk(bk) + [csk], w=[("scr", ab + 1)])
                        S.op("pool", lambda e, A=A, B=B, dst=dst: e.tensor_tensor(out=dst, in0=A, in1=B, op=ALU.add), r=[("scr", ab), ("scr", ab + 1)], w=[dkey])

        B_T = (0, 1)
        B_O = (2, 3)
        B_SC = 4
        B_KV = 5
        B_MIX = ((6, 7), (4, 5))
        tb_ctr = [0]

        def next_tbank():
            b = B_T[tb_ctr[0] % 2]
            tb_ctr[0] += 1
            return b

        def v_blk(t, j, wv, wk):
            bk = 6 + (j % 2)
            for kc in range(8):
                S.op("pe", lambda e, j=j, kc=kc, bk=bk, wv=wv: e.matmul(psf(bk), lhsT=htv[:, kc, j * 128:(j + 1) * 128], rhs=wv[:, kc, :], start=(kc == 0), stop=(kc == 7)),
                     r=[wk, ("hT", kc, j)], w=psk(bk))
            S.op("dve", lambda e, bk=bk, j=j: e.tensor_tensor(out=VD[:, j * 512:(j + 1) * 512].rearrange("p (h e) -> p h e", h=4),
                                                              in0=psf(bk).rearrange("p (h e) -> p h e", h=4),
                                                              in1=VDEC[:].unsqueeze(2).broadcast_to([128, 4, 128]), op=ALU.mult),
                 r=psk(bk) + ["vdec"], w=[("vd", j)])

        def g_blk(t, j, wv, wk):
            bk = 6 + (j % 2)
            for kc in range(8):
                S.op("pe", lambda e, j=j, kc=kc, bk=bk, wv=wv: e.matmul(psf(bk), lhsT=htv[:, kc, j * 128:(j + 1) * 128], rhs=wv[:, kc, :], start=(kc == 0), stop=(kc == 7)),
                     r=[wk, ("hT", kc, j)], w=psk(bk))
            S.op("act", lambda e, bk=bk, j=j: e.activation(out=SG[:, j * 512:(j + 1) * 512], in_=psf(bk), func=AF.Silu), r=psk(bk), w=[("sg", j)])

        def ret_AT(t, j):
            jg = t * 4 + j
            first = (jg % 16 == 0)
            bTP = next_tbank()
            qkt = QKT[j % 2]
            qk = ("qkt", j % 2)
            tpb = PSB[:, bTP * 1024:(bTP + 1) * 1024]
            for h in range(4):
                S.op("pe", lambda e, h=h, j=j, tpb=tpb: e.transpose(out=tpb[:, h * 128:(h + 1) * 128], in_=QR[:, j * 512 + h * 128: j * 512 + (h + 1) * 128], identity=IDB[:]),
                     r=[("qr", j), "idb"], w=psk(bTP))
            for h in range(4):
                S.op("pe", lambda e, h=h, j=j, tpb=tpb: e.transpose(out=tpb[:, (4 + h) * 128:(5 + h) * 128], in_=KR[:, j * 512 + h * 128: j * 512 + (h + 1) * 128], identity=IDB[:]),
                     r=[("kr", j), "idb"], w=psk(bTP))
            S.op("dve", lambda e, qkt=qkt, tpb=tpb: e.tensor_copy(out=qkt[:], in_=tpb), r=psk(bTP), w=[qk])

        def ret_AK(t, j):
            jg = t * 4 + j
            first = (jg % 16 == 0)
            bKV = B_KV
            if jg % 16 != 15:
                for h in range(4):
                    hs = slice(h * 128, (h + 1) * 128)
                    S.op("pe", lambda e, hs=hs, j=j, bKV=bKV: e.matmul(PS[:, bKV * 512 + hs.start: bKV * 512 + hs.stop], lhsT=KR[:, j * 512 + hs.start: j * 512 + hs.stop],
                                                                      rhs=VD[:, j * 512 + hs.start: j * 512 + hs.stop], start=True, stop=True),
                         r=[("kr", j), ("vd", j)], w=psk(bKV))
                if first:
                    S.op("dve", lambda e, bKV=bKV: e.tensor_copy(out=STATE[:], in_=psf(bKV)), r=psk(bKV), w=[("state", hh) for hh in range(4)])
                else:
                    for h in range(4):
                        hs = slice(h * 128, (h + 1) * 128)
                        S.op("dve", lambda e, hs=hs, h=h, bKV=bKV: e.scalar_tensor_tensor(out=STATE[:, hs], in0=STATE[:, hs], scalar=float(GAMMA[h] ** 128),
                                                                                         in1=PS[:, bKV * 512 + hs.start: bKV * 512 + hs.stop], op0=ALU.mult, op1=ALU.add),
                             r=psk(bKV) + [("state", h)], w=[("state", h)])
                nsb = SBF[(jg + 1) % 2]
                S.op("dve", lambda e, nsb=nsb: e.tensor_copy(out=nsb[:], in_=STATE[:]), r=[("state", hh) for hh in range(4)], w=[("sbf", (jg + 1) % 2)])

        def ret_B(t, j):
            bSC = B_SC
            qkt = QKT[j % 2]
            qk = ("qkt", j % 2)
            for h in range(4):
                S.op("pe", lambda e, h=h, qkt=qkt, bSC=bSC: e.matmul(PS[:, bSC * 512 + h * 128: bSC * 512 + (h + 1) * 128], lhsT=qkt[:, (4 + h) * 128:(5 + h) * 128],
                                                                    rhs=qkt[:, h * 128:(h + 1) * 128], start=True, stop=True), r=[qk], w=psk(bSC))
            st = ST[j % 2]
            S.op("dve", lambda e, st=st, bSC=bSC: e.tensor_tensor(out=st[:], in0=psf(bSC), in1=DMASK[:], op=ALU.mult), r=psk(bSC) + ["dmask"], w=[("st", j % 2)])

        def ret_C(t, j):
            jg = t * 4 + j
            first = (jg % 16 == 0)
            bO = B_O[j % 2]
            qkt = QKT[j % 2]
            qk = ("qkt", j % 2)
            st = ST[j % 2]
            sk = ("st", j % 2)
            sbf = SBF[jg % 2]
            sbk = ("sbf", jg % 2)
            for h in range(4):
                hs = slice(h * 128, (h + 1) * 128)
                S.op("pe", lambda e, hs=hs, st=st, j=j, bO=bO, first=first: e.matmul(PS[:, bO * 512 + hs.start: bO * 512 + hs.stop], lhsT=st[:, hs],
                                                                                   rhs=VD[:, j * 512 + hs.start: j * 512 + hs.stop], start=True, stop=first),
                     r=[sk, ("vd", j)], w=psk(bO))
                if not first:
                    S.op("pe", lambda e, hs=hs, qkt=qkt, sbf=sbf, bO=bO: e.matmul(PS[:, bO * 512 + hs.start: bO * 512 + hs.stop], lhsT=qkt[:, hs], rhs=sbf[:, hs],
                                                                                 start=False, stop=True), r=[qk, sbk], w=psk(bO))
            sto, ko = new_stat()
            for h in range(4):
                hs = slice(h * 128, (h + 1) * 128)
                S.op("act", lambda e, hs=hs, h=h, sto=sto, bO=bO: e.activation(out=JUNK[:, hs], in_=PS[:, bO * 512 + hs.start: bO * 512 + hs.stop], func=AF.Square,
                                                                              scale=float(128.0 ** -0.5), accum_out=sto[:, h:h + 1]), r=psk(bO), w=[ko])
            rs, rk = rstd_ops(sto[:, 0:4], ko, eps_ap=EPSR[:])
            yr = YR[j % 2]
            for h in range(4):
                hs = slice(h * 128, (h + 1) * 128)
                S.op("dve", lambda e, hs=hs, h=h, bO=bO, rs=rs, yr=yr, j=j: e.scalar_tensor_tensor(out=yr[:, hs], in0=PS[:, bO * 512 + hs.start: bO * 512 + hs.stop],
                                                                                                scalar=rs[:, h:h + 1], in1=SG[:, j * 512 + hs.start: j * 512 + hs.stop],
                                                                                                op0=ALU.mult, op1=ALU.mult),
                     r=psk(bO) + [rk, ("sg", j)], w=[("yr", j % 2)])

        def ret_D(t, j):
            bTP = next_tbank()
            tpb = PSB[:, bTP * 1024:(bTP + 1) * 1024]
            yr = YR[j % 2]
            yk = ("yr", j % 2)
            for h in range(4):
                S.op("pe", lambda e, h=h, yr=yr, tpb=tpb: e.transpose(out=tpb[:, h * 128:(h + 1) * 128], in_=yr[:, h * 128:(h + 1) * 128], identity=IDB[:]),
                     r=[yk, "idb"], w=psk(bTP))
            S.op("act", lambda e, j=j, tpb=tpb: e.activation(out=ytv[:, 4:8, j * 128:(j + 1) * 128], in_=tpb[:, 0:512].rearrange("p (h n) -> p h n", h=4), func=AF.Copy),
                 r=psk(bTP), w=[("yTr", j)])

        pn_store = {}

        def postnorm_a(t, j, b0, ggi, half):
            mix = PS[:, b0 * 512: b0 * 512 + 1024]
            st, k = new_stat()
            S.op("act", lambda e, st=st: e.activation(out=JUNK[:], in_=mix, func=AF.Square, scale=1.0 / 32.0, accum_out=st[:, 0:1]), r=psk(b0, b0 + 1), w=[k])
            rs, rk = rstd_ops(st[:, 0:1], k)
            TMP = SCR[:, half * 1024:(half + 1) * 1024]
            tk = [("scr", 2 * half), ("scr", 2 * half + 1)]
            S.op("dve", lambda e, TMP=TMP: e.tensor_tensor(out=TMP, in0=mix, in1=GG[ggi][:], op=ALU.mult), r=psk(b0, b0 + 1) + [("gg", ggi), k], w=tk)
            pn_store[(t, j, ggi)] = (rs, rk, TMP, tk)

        def postnorm_b(t, j, ggi, store):
            s = xslot(t, j)
            X = XS[s]
            rs, rk, TMP, tk = pn_store.pop((t, j, ggi))
            S.op("dve", lambda e, X=X, TMP=TMP, rs=rs: e.scalar_tensor_tensor(out=X[:], in0=TMP, scalar=rs, in1=X[:], op0=ALU.mult, op1=ALU.add),
                 r=tk + [rk] + xk(s), w=xk(s))
            if store:
                S.dma("sp", out_d[x_rows(t, j), :], X[:], r=xk(s), w=[], key=("out", s))

        def out_blk(t, j, wa, wak, wb, wbk):
            b0 = B_MIX[j % 2][0]
            for half in range(2):
                for kc in range(8):
                    wv, wk = (wa, wak) if kc < 4 else (wb, wbk)
                    lk = ("yTc", kc) if kc < 4 else ("yTr", j)
                    S.op("pe", lambda e, j=j, kc=kc, half=half, wv=wv, b0=b0: e.matmul(psf(b0 + half), lhsT=ytv[:, kc, j * 128:(j + 1) * 128],
                                                                                     rhs=wv[:, kc % 4, half * 512:(half + 1) * 512], start=(kc == 0), stop=(kc == 7)),
                         r=[wk, lk], w=psk(b0 + half))
            postnorm_a(t, j, b0, 0, j % 2)

        rs_store = {}

        def pre_ssq(t, sub, j):
            s = xslot(t, j)
            X = XS[s]
            st, k = new_stat()
            S.op("act", lambda e, X=X, st=st: e.activation(out=JUNK[:], in_=X[:], func=AF.Square, scale=1.0 / 32.0, accum_out=st[:, 0:1]), r=xk(s), w=[k])
            rs_store[(t, sub, j)] = rstd_ops(st[:, 0:1], k)

        def pre_xn(t, sub, j):
            s = xslot(t, j)
            X = XS[s]
            rs, rk = rs_store.pop((t, sub, j))
            xn = XN[j]
            if sub == 1 and os.environ.get("K_XNPOOL", "0") == "1":
                S.op("pool", lambda e, X=X, xn=xn, rs=rs: e.tensor_scalar(out=xn[:], in0=X[:], scalar1=rs, scalar2=None, op0=ALU.mult), r=xk(s) + [rk], w=[("xn", j)])
            else:
                S.op("act", lambda e, X=X, xn=xn, rs=rs: e.activation(out=xn[:], in_=X[:], func=AF.Copy, scale=rs), r=xk(s) + [rk], w=[("xn", j)])

        def pre_act(t, sub, j):
            pre_ssq(t, sub, j)
            pre_xn(t, sub, j)

        DMAT = os.environ.get("K_DMAT", "0") == "1"

        def pre_dmat(t, sub, j):
            b = t // 4
            gs = g3(GST[sub])
            sh = SHT[sub]
            xn = XN[j]
            keys = [("hT", kc, j) for kc in range(8)]
            for kc in range(8):
                S.dma_t("sp", htv[:, kc, j * 128:(j + 1) * 128], xn[:, kc * 128:(kc + 1) * 128], r=[("xn", j)],
                        w=keys if kc in (0, 7) else [], key=("tpd", j))
            for kc in range(8):
                dst = htv[:, kc, j * 128:(j + 1) * 128]
                if kc % 2 == 0:
                    S.op("act", lambda e, kc=kc, dst=dst: e.activation(out=dst, in_=dst, func=AF.Identity, bias=sh[:, kc, b:b + 1], scale=gs[:, kc, b:b + 1]),
                         r=[("hT", kc, j), "gst%d" % sub, "modt"], w=[("hT", kc, j)])
                else:
                    S.op("dve", lambda e, kc=kc, dst=dst: e.tensor_scalar(out=dst, in0=dst, scalar1=gs[:, kc, b:b + 1], scalar2=sh[:, kc, b:b + 1], op0=ALU.mult, op1=ALU.add),
                         r=[("hT", kc, j), "gst%d" % sub, "modt"], w=[("hT", kc, j)])

        def pre_tp_big(t, sub, tpbase):
            tp = PSB[:, tpbase * 1024: tpbase * 1024 + 4096].rearrange("p (k n) -> p k n", k=8)
            tpk = psk(tpbase, tpbase + 1, tpbase + 2, tpbase + 3)
            for j in range(4):
                xn = XN[j]
                for kc in range(8):
                    S.op("pe", lambda e, xn=xn, kc=kc, j=j: e.transpose(out=tp[:, kc, j * 128:(j + 1) * 128], in_=xn[:, kc * 128:(kc + 1) * 128], identity=IDB[:]),
                         r=[("xn", j), "idb"], w=[("ps", tpbase + kc // 2)])

        def pre_evac_big(t, sub, tpbase):
            b = t // 4
            tp = PSB[:, tpbase * 1024: tpbase * 1024 + 4096].rearrange("p (k n) -> p k n", k=8)
            gs = g3(GST[sub])
            sh = SHT[sub]
            for kc in range(8):
                bkk = psk(tpbase + kc // 2)
                if (kc // 2) % 2 == 0:
                    S.op("act", lambda e, kc=kc: e.activation(out=htv[:, kc, :], in_=tp[:, kc, :], func=AF.Identity, bias=sh[:, kc, b:b + 1], scale=gs[:, kc, b:b + 1]),
                         r=bkk + ["gst%d" % sub, "modt"], w=hk(kc))
                else:
                    S.op("dve", lambda e, kc=kc: e.tensor_scalar(out=htv[:, kc, :], in0=tp[:, kc, :], scalar1=gs[:, kc, b:b + 1], scalar2=sh[:, kc, b:b + 1], op0=ALU.mult, op1=ALU.add),
                         r=bkk + ["gst%d" % sub, "modt"], w=hk(kc))

        def pre_tp_blk(t, sub, j):
            b = t // 4
            gs = g3(GST[sub])
            sh = SHT[sub]
            xn = XN[j]
            for half in range(2):
                bTP = B_T[half]
                tpb = PSB[:, bTP * 1024:(bTP + 1) * 1024]
                for kq in range(4):
                    kc = half * 4 + kq
                    S.op("pe", lambda e, xn=xn, kc=kc, kq=kq, tpb=tpb: e.transpose(out=tpb[:, kq * 128:(kq + 1) * 128], in_=xn[:, kc * 128:(kc + 1) * 128], identity=IDB[:]),
                         r=[("xn", j), "idb"], w=psk(bTP))
            for kq in range(4):
                kc = kq
                tpb = PSB[:, B_T[0] * 1024:(B_T[0] + 1) * 1024]
                S.op("act", lambda e, kc=kc, kq=kq, j=j, tpb=tpb: e.activation(out=htv[:, kc, j * 128:(j + 1) * 128], in_=tpb[:, kq * 128:(kq + 1) * 128], func=AF.Identity,
                                                                              bias=sh[:, kc, b:b + 1], scale=gs[:, kc, b:b + 1]),
                     r=psk(B_T[0]) + ["gst%d" % sub, "modt"], w=[("hT", kc, j)])
                kc = 4 + kq
                tpb1 = PSB[:, B_T[1] * 1024:(B_T[1] + 1) * 1024]
                S.op("dve", lambda e, kc=kc, kq=kq, j=j, tpb1=tpb1: e.tensor_scalar(out=htv[:, kc, j * 128:(j + 1) * 128], in0=tpb1[:, kq * 128:(kq + 1) * 128],
                                                                                   scalar1=gs[:, kc, b:b + 1], scalar2=sh[:, kc, b:b + 1], op0=ALU.mult, op1=ALU.add),
                     r=psk(B_T[1]) + ["gst%d" % sub, "modt"], w=[("hT", kc, j)])

        def fc1_evac(oc, bk):
            R = SCR[:, (oc % 4) * 512:(oc % 4 + 1) * 512]
            rk = ("scr", oc % 4)
            S.op("act", lambda e, bk=bk, R=R: e.activation(out=R, in_=psf(bk), func=AF.Relu), r=psk(bk), w=[rk])
            S.op("dve" if oc % 2 == 0 else "pool", lambda e, oc=oc, R=R: e.tensor_tensor(out=ftv[:, oc, :], in0=R, in1=R, op=ALU.mult), r=[rk], w=[("fT", oc)])

        def fc1_stage(t):
            nxt = t + 1 < NT
            wg = [next_weight(), next_weight()]
            split_banks = [2, 3, 6, 7, 4, 5]
            for oc in range(6):
                wv, wk = wg[oc // 4]
                i = oc % 4
                bk = split_banks[oc]
                for kc in range(8):
                    S.op("pe", lambda e, i=i, kc=kc, bk=bk, wv=wv: e.matmul(PS[:, bk * 512: bk * 512 + 256], lhsT=wv[:, kc, i * 128:(i + 1) * 128], rhs=htv[:, kc, 0:256],
                                                                          start=(kc == 0), stop=(kc == 7)),
                         r=[wk] + [("hT", kc, jj) for jj in range(2)], w=psk(bk))
            (pre_dmat if DMAT else pre_tp_blk)(t, 1, 2)
            (pre_dmat if DMAT else pre_tp_blk)(t, 1, 3)
            for oc in range(6):
                wv, wk = wg[oc // 4]
                i = oc % 4
                bk = split_banks[oc]
                for kc in range(8):
                    S.op("pe", lambda e, i=i, kc=kc, bk=bk, wv=wv: e.matmul(PS[:, bk * 512 + 256: bk * 512 + 512], lhsT=wv[:, kc, i * 128:(i + 1) * 128], rhs=htv[:, kc, 256:512],
                                                                          start=(kc == 0), stop=(kc == 7)),
                         r=[wk, ("hT", kc, 2), ("hT", kc, 3)], w=psk(bk))
                fc1_evac(oc, bk)
            if nxt:
                pre_act(t + 1, 0, 0)
            for oc in range(6, 32):
                g = oc // 4
                i = oc % 4
                if g >= 2 and i == 0:
                    wg.append(next_weight())
                wv, wk = wg[g]
                bk = bank_ctr[0] % 4; bank_ctr[0] += 1
                for kc in range(8):
                    S.op("pe", lambda e, i=i, kc=kc, bk=bk, wv=wv: e.matmul(psf(bk), lhsT=wv[:, kc, i * 128:(i + 1) * 128], rhs=htv[:, kc, :], start=(kc == 0), stop=(kc == 7)),
                         r=[wk] + hk(kc), w=psk(bk))
                fc1_evac(oc, bk)
                if nxt and i == 3 and g in (1, 2, 3):
                    pre_act(t + 1, 0, g)
                if nxt and i == 3 and g == 5 and not DMAT:
                    pre_tp_big(t + 1, 0, 4)
            if nxt:
                if DMAT:
                    for j in range(4):
                        pre_dmat(t + 1, 0, j)
                else:
                    pre_evac_big(t + 1, 0, 4)

        def fc2_stage(t):
            for g in range(8):
                wv, wk = next_weight()
                for jp in (((0,), (1,), (2,), (3,)) if g == 7 else ((0, 1), (2, 3))):
                    for ki in range(4):
                        kc = g * 4 + ki
                        for j in jp:
                            for half in range(2):
                                S.op("pe", lambda e, j=j, kc=kc, ki=ki, half=half, wv=wv: e.matmul(psf(2 * j + half), lhsT=ftv[:, kc, j * 128:(j + 1) * 128],
                                                                                                 rhs=wv[:, ki, half * 512:(half + 1) * 512], start=(kc == 0), stop=(kc == 31)),
                                     r=[wk, ("fT", kc)], w=psk(2 * j + half))
            postnorm_a(t, 0, 0, 1, 0)
            postnorm_a(t, 1, 2, 1, 1)
            postnorm_b(t, 0, 1, True)
            postnorm_a(t, 2, 4, 1, 0)
            postnorm_b(t, 1, 1, True)
            postnorm_a(t, 3, 6, 1, 1)
            postnorm_b(t, 2, 1, True)
            postnorm_b(t, 3, 1, True)

        load_x(0)
        for t in range(NT):
            request_weights(t)
        rot_tables(0)
        for j in range(4):
            pre_act(0, 0, j)
        adaln_stage(0, 3)
        pre_tp_big(0, 0, 4)
        adaln_stage(3, 12)
        pre_evac_big(0, 0, 4)
        for r in range(8 if KPRO & 1 else 0):
            S.dma("pool", w1b[r * 128:(r + 1) * 128, :], w1_d[r * 128:(r + 1) * 128, :], r=["modt"], w=[("wb", "w1")] if r == 7 else [], key="cast_w1")
        for t in range(NT):
            if t % 4 == 0:
                S.op("pool", lambda e: e.memset(uv[:, :, 0:2], 0.0), w=[("u", c) for c in range(4)])
            if t + 1 < NT:
                load_x(t + 1)
            conv_stage(t)
            if t % 4 == 0:
                seq_setup(t)
            qkv_stage(t)
            if t == 0:
                for r in range(8 if KPRO & 1 else 0):
                    S.dma("pool", w2b[r * 512:(r + 1) * 512, :], w2_d[r * 512:(r + 1) * 512, :], r=["modt"], w=[("wb", "w2")] if r == 7 else [], key="cast_w2")
            vw, vwk = next_weight()
            gw, gwk = next_weight()
            ret_AT(t, 0); v_blk(t, 0, vw, vwk)
            ret_AT(t, 1); v_blk(t, 1, vw, vwk)
            ret_B(t, 0); v_blk(t, 2, vw, vwk)
            ret_B(t, 1); v_blk(t, 3, vw, vwk)
            ret_AK(t, 0); g_blk(t, 0, gw, gwk)
            ret_C(t, 0); g_blk(t, 1, gw, gwk)
            ret_AK(t, 1); ret_AT(t, 2); g_blk(t, 2, gw, gwk)
            ret_B(t, 2); ret_C(t, 1); ret_AK(t, 2); ret_AT(t, 3); g_blk(t, 3, gw, gwk)
            ret_B(t, 3); ret_D(t, 0); ret_C(t, 2); ret_AK(t, 3)
            wa, wak = next_weight()
            wb, wbk = next_weight()
            ret_D(t, 1); ret_C(t, 3)
            out_blk(t, 0, wa, wak, wb, wbk)
            ret_D(t, 2)
            postnorm_b(t, 0, 0, False)
            pre_ssq(t, 1, 0)
            out_blk(t, 1, wa, wak, wb, wbk)
            pre_xn(t, 1, 0)
            postnorm_b(t, 1, 0, False)
            pre_ssq(t, 1, 1)
            out_blk(t, 2, wa, wak, wb, wbk)
            ret_D(t, 3)
            (pre_dmat if DMAT else pre_tp_blk)(t, 1, 0)
            pre_xn(t, 1, 1)
            postnorm_b(t, 2, 0, False)
            pre_ssq(t, 1, 2)
            out_blk(t, 3, wa, wak, wb, wbk)
            (pre_dmat if DMAT else pre_tp_blk)(t, 1, 1)
            pre_xn(t, 1, 2)
            postnorm_b(t, 3, 0, False)
            pre_ssq(t, 1, 3)
            pre_xn(t, 1, 3)
            fc1_stage(t)
            if t + 1 < NT:
                rot_tables(t + 1)
            fc2_stage(t)
        S.emit(nc, es)
    return nc


def _consts():
    ident = np.eye(128, dtype=np.float32)
    invf = (np.float32(10000.0) ** (-(np.arange(64, dtype=np.float32)) / np.float32(64))).astype(np.float32)
    invf = np.ascontiguousarray(np.broadcast_to(invf[None, :], (128, 64)))
    m = np.arange(128)[:, None].astype(np.float64)
    c = np.arange(128)[None, :].astype(np.float64)
    same = (m // 64) == (c // 64)
    causal_x = (m // 64 == 0) & (c // 64 == 1)
    dmask = np.zeros((128, 4, 128), np.float64)
    vdec = np.zeros((128, 4), np.float64)
    epsr = np.zeros((128, 4), np.float64)
    for h in range(4):
        lg = np.log1p(-2.0 ** (-5 - h))
        dd = np.where(same, np.exp(lg * np.abs(c - m)), np.where(causal_x, np.exp(lg * (c - m)), 0.0))
        dmask[:, h, :] = dd * np.exp(-lg * (128.0 + c - m))
        vdec[:, h] = np.exp(lg * (127.0 - np.arange(128)))
        epsr[:, h] = EPS * 128.0 * np.exp(-2.0 * lg * (np.arange(128) + 1.0))
    return {
        "ident": ident, "invf": invf, "dmask": np.ascontiguousarray(dmask.reshape(128, 512).astype(np.float32)),
        "vdec": vdec.astype(np.float32), "epsr": epsr.astype(np.float32),
    }


def make_in_maps(inputs, n_cores=8, NT=16):
    f = lambda a: np.ascontiguousarray(np.asarray(a))
    x = f(inputs["x"]); c = f(inputs["c"]); pos = f(inputs["positions"])
    shared = {
        "w_ada": f(inputs["w_ada"])[0], "b_ada": f(inputs["b_ada"])[0].reshape(48, 128),
        "g_pre_mix": f(inputs["g_pre_mix"])[0].reshape(8, 128), "g_post_mix": f(inputs["g_post_mix"])[0].reshape(8, 128),
        "w_in": f(inputs["w_in"])[0], "conv_w": f(inputs["conv_w"])[0].reshape(12, 128), "w_out": f(inputs["w_out"])[0],
        "g_pre_mlp": f(inputs["g_pre_mlp"])[0].reshape(8, 128), "g_post_mlp": f(inputs["g_post_mlp"])[0].reshape(8, 128),
        "w_fc1": f(inputs["w_fc1"])[0], "w_fc2": f(inputs["w_fc2"])[0],
    }
    shared.update(_consts())
    maps = []
    for i in range(n_cores):
        bs = slice(i * NB_CORE, (i + 1) * NB_CORE)
        m = dict(shared)
        m["x"] = np.ascontiguousarray(x[bs].reshape(NB_CORE * SEQ, D)[: NT * 512])
        m["c"] = np.ascontiguousarray(c[bs].reshape(32, 128))
        m["pos"] = np.ascontiguousarray(pos[bs].reshape(64, 128).astype(np.int32))
        maps.append(m)
    return maps


_PROG = {}


def kernel(**inputs):
    if 16 not in _PROG:
        _PROG[16] = build_program(16)
    nc = _PROG[16]
    maps = make_in_maps(inputs, 8, 16)
    res = run_bass_kernel_spmd(nc, maps, core_ids=list(range(8)))
    outs = [np.asarray(r["out"]).reshape(NB_CORE, SEQ, D) for r in res.results]
    return np.concatenate(outs, axis=0).astype(np.float32, copy=False)
```

```python
import os
import numpy as np
from contextlib import ExitStack
import concourse.bass as bass
import concourse.mybir as mybir
from concourse.bass_utils import run_bass_kernel_spmd

F32 = mybir.dt.float32
BF16 = mybir.dt.bfloat16
I32 = mybir.dt.int32
ALU = mybir.AluOpType
AF = mybir.ActivationFunctionType
AX = mybir.AxisListType
PI = float(np.pi)

D = 1024
SEQ = 2048
NB_CORE = 4
DFF = 4096
EPS = 1e-6
NWSLOT = 5
XSLOTS = 8
GAMMA = [1.0 - 2.0 ** (-5 - h) for h in range(4)]


class _Op:
    __slots__ = ("eng", "fn", "deps", "raw", "is_dma", "semkey", "val", "sig", "idx")


class Sched:
    ENG = ("pe", "act", "dve", "pool", "sp")

    def __init__(self):
        self.ops = []
        self.last_w = {}
        self.readers = {}
        self.dma_cnt = {}

    def _add(self, eng, fn, r, w, is_dma, semkey):
        op = _Op()
        op.eng, op.fn, op.is_dma, op.semkey = eng, fn, is_dma, semkey
        op.idx = len(self.ops)
        op.sig = is_dma
        deps = {}
        for b in r:
            lw = self.last_w.get(b)
            if lw is not None:
                deps[lw] = True
        for b in w:
            lw = self.last_w.get(b)
            if lw is not None:
                deps.setdefault(lw, b in r)
            last_rd = {}
            for rd in self.readers.get(b, ()):
                if rd == op.idx:
                    continue
                rop = self.ops[rd]
                if rop.is_dma:
                    deps.setdefault(rd, False)
                else:
                    last_rd[rop.eng] = max(last_rd.get(rop.eng, -1), rd)
            for rd in last_rd.values():
                deps.setdefault(rd, False)
        op.deps = deps
        for b in r:
            if b not in w:
                self.readers.setdefault(b, []).append(op.idx)
        for b in w:
            self.last_w[b] = op.idx
            self.readers[b] = []
        if is_dma:
            self.dma_cnt[semkey] = self.dma_cnt.get(semkey, 0) + 16
            op.val = self.dma_cnt[semkey]
        self.ops.append(op)
        return op

    def op(self, eng, fn, r=(), w=()):
        return self._add(eng, fn, tuple(r), tuple(w), False, None)

    def dma(self, eng, out, in_, r=(), w=(), key=None):
        return self._add(eng, (lambda e, o=out, i=in_: e.dma_start(out=o, in_=i)), tuple(r), tuple(w), True, key)

    def dma_t(self, eng, out, in_, r=(), w=(), key=None):
        return self._add(eng, (lambda e, o=out, i=in_: e.dma_start_transpose(out=o, in_=i)), tuple(r), tuple(w), True, key)

    def emit(self, nc, es):
        ops = self.ops
        need = []
        for op in ops:
            lst = []
            for d, raw in op.deps.items():
                dop = ops[d]
                if dop.is_dma:
                    lst.append(d)
                elif dop.eng == op.eng:
                    if raw and op.eng != "pe" and not op.is_dma:
                        lst.append(d)
                    elif op.is_dma:
                        lst.append(d)
                else:
                    lst.append(d)
            need.append(lst)
            for d in lst:
                ops[d].sig = True
        cnt = {e: 0 for e in self.ENG}
        for op in ops:
            if not op.is_dma and op.sig:
                cnt[op.eng] += 1
                op.val = cnt[op.eng]
        sems = {}
        for e in self.ENG:
            sems[e] = es.enter_context(nc.semaphore("sem_" + e))
        for k in self.dma_cnt:
            sems[("dma", k)] = es.enter_context(nc.semaphore("dsem_%d" % len(sems)))
        block = es.enter_context(nc.Block())
        per_eng = {e: [op for op in ops if op.eng == e] for e in self.ENG}
        final_dma = dict(self.dma_cnt)

        def run(e, eng):
            waited = {}
            for op in per_eng[e]:
                for d in need[op.idx]:
                    dop = ops[d]
                    sk = ("dma", dop.semkey) if dop.is_dma else dop.eng
                    if waited.get(sk, 0) >= dop.val:
                        continue
                    eng.wait_ge(sems[sk], dop.val)
                    waited[sk] = dop.val
                ins = op.fn(eng)
                if op.sig:
                    if op.is_dma:
                        ins.then_inc(sems[("dma", op.semkey)], 16)
                    else:
                        ins.then_inc(sems[e], 1)
            if e == "sp":
                for k, v in final_dma.items():
                    if isinstance(k, tuple) and k[0] == "out":
                        eng.wait_ge(sems[("dma", k)], v)

        @block.tensor
        def _(eng):
            run("pe", eng)

        @block.scalar
        def _(eng):
            run("act", eng)

        @block.vector
        def _(eng):
            run("dve", eng)

        @block.gpsimd
        def _(eng):
            run("pool", eng)

        @block.sync
        def _(eng):
            run("sp", eng)


def build_program(NT=16):
    nc = bass.Bass("TRN2", target_bir_lowering=False)
    NTOK = NT * 512

    def din(name, shape, dt=F32):
        return nc.dram_tensor(name, shape, dt, kind="ExternalInput").ap()

    x_d = din("x", [NTOK, D])
    c_d = din("c", [32, 128])
    pos_d = din("pos", [64, 128], I32)
    wada_d = din("w_ada", [D, 6 * D])
    bada_d = din("b_ada", [48, 128])
    gpm_d = din("g_pre_mix", [8, 128])
    gqm_d = din("g_post_mix", [8, 128])
    win_d = din("w_in", [D, 3584])
    cw_d = din("conv_w", [12, 128])
    wout_d = din("w_out", [D, D])
    gpl_d = din("g_pre_mlp", [8, 128])
    gql_d = din("g_post_mlp", [8, 128])
    w1_d = din("w_fc1", [D, DFF])
    w2_d = din("w_fc2", [DFF, D])
    ident_d = din("ident", [128, 128])
    invf_d = din("invf", [128, 64])
    dmask_d = din("dmask", [128, 512])
    vdec_d = din("vdec", [128, 4])
    epsr_d = din("epsr", [128, 4])
    out_d = nc.dram_tensor("out", [NTOK, D], F32, kind="ExternalOutput").ap()
    winb = nc.dram_tensor("winb", [D, 3584], BF16).ap()
    woutb = nc.dram_tensor("woutb", [D, D], BF16).ap()
    w1b = nc.dram_tensor("w1b", [D, DFF], BF16).ap()
    w2b = nc.dram_tensor("w2b", [DFF, D], BF16).ap()

    S = Sched()
    with ExitStack() as es:
        def sb(name, shape, dt):
            return es.enter_context(nc.sbuf_tensor(name, shape, dt))

        XS = [sb("xs%d" % i, [128, D], F32) for i in range(XSLOTS)]
        XN = [sb("xn%d" % i, [128, D], BF16) for i in range(4)]
        JUNK = sb("junk", [128, D], BF16)
        HT = sb("hT", [128, 8 * 512], BF16)
        XY = sb("xy", [128, 4 * 512], F32)
        U = sb("u", [128, 4 * 514], F32)
        QR = sb("qr", [128, 4 * 512], BF16)
        KR = sb("kr", [128, 4 * 512], BF16)
        VD = sb("vd", [128, 4 * 512], BF16)
        SG = sb("sg", [128, 4 * 512], BF16)
        QKT = [sb("qkt%d" % i, [128, 1024], BF16) for i in range(2)]
        ST = [sb("st%d" % i, [128, 512], BF16) for i in range(2)]
        STATE = sb("state", [128, 512], F32)
        SBF = [sb("sbf%d" % i, [128, 512], BF16) for i in range(2)]
        YR = [sb("yr%d" % i, [128, 512], BF16) for i in range(2)]
        YT = sb("yT", [128, 8 * 512], BF16)
        SCR = sb("scr", [128, 4 * 512], F32)
        FT = sb("fT", [128, 32 * 512], BF16)
        GG = [sb("gg%d" % i, [128, D], F32) for i in range(2)]
        CS = [sb("cs0", [128, 3 * 256], F32)] * 2
        TT0 = sb("tt0", [128, 256], F32)
        TT1 = sb("tt1", [128, 256], F32)
        TTI = sb("tti", [128, 256], I32)
        TT3 = sb("tt3", [128, 256], F32)
        US = sb("us", [128, 512], F32)
        DMASK = sb("dmask_s", [128, 512], F32)
        VDEC = sb("vdec_s", [128, 4], F32)
        EPSR = sb("epsr_s", [128, 4], F32)
        INVF = sb("invf_s", [128, 64], F32)
        IDF = sb("idf", [128, 128], F32)
        IDB = sb("idb", [128, 128], BF16)
        VROWS = sb("vrows", [128, 128], F32)
        VT = sb("vt", [128, 124], F32)
        SCT = sb("sct", [128, 32], F32)
        MODT = sb("modt", [128, 48 * 4], F32)
        GST = [sb("gst%d" % i, [128, 32], F32) for i in range(2)]
        GGT = [sb("ggt%d" % i, [128, 32], F32) for i in range(2)]
        POSI = sb("posi", [64, 128], I32)
        POSF = sb("posf", [64, 128], F32)
        POST = sb("post", [128, 64], F32)
        NHALF = sb("nhalf", [128, 4], F32)
        STAT = [sb("stat%d" % i, [128, 8], F32) for i in range(8)]
        WS = [sb("ws%d" % i, [128, 4096], BF16) for i in range(NWSLOT)]
        PS = es.enter_context(nc.psum_tensor("PS", [128, 4096], F32))
        PSB = PS[:].bitcast(BF16)

        def psf(bank, n=512):
            return PS[:, bank * 512: bank * 512 + n]

        def psk(*banks):
            return [("ps", b) for b in banks]

        stat_ctr = [0]

        def new_stat():
            i = stat_ctr[0] % 8
            stat_ctr[0] += 1
            return STAT[i], ("stat", i)

        SLOTCAST = os.environ.get("K_SLOTCAST", "1") == "1"
        KPRO = int(os.environ.get("K_PRO", "3"))
        if SLOTCAST:
            KPRO &= ~1
        for r in range(8 if KPRO & 1 else 0):
            S.dma("pool", winb[r * 128:(r + 1) * 128, :], win_d[r * 128:(r + 1) * 128, :], w=[("wb", "win")] if r == 7 else [], key="cast_win")
        for r in range(2 if KPRO & 1 else 0):
            S.dma("pool", woutb[r * 512:(r + 1) * 512, :], wout_d[r * 512:(r + 1) * 512, :], w=[("wb", "wout")] if r == 1 else [], key="cast_wout")
        loads = [
            (VROWS[0:32, :], c_d), (VROWS[32:80, :], bada_d), (VROWS[80:88, :], gpm_d), (VROWS[88:96, :], gqm_d),
            (VROWS[96:104, :], gpl_d), (VROWS[104:112, :], gql_d), (VROWS[112:124, :], cw_d),
            (POSI[:], pos_d), (IDF[:], ident_d), (INVF[:], invf_d), (DMASK[:], dmask_d), (VDEC[:], vdec_d), (EPSR[:], epsr_d),
        ]
        crit = loads[0:7] + [loads[8]]
        rest = [loads[7]] + loads[9:]
        for i, (o, s) in enumerate(crit):
            S.dma("sp", o, s, w=["vrows", "idf"] if i == len(crit) - 1 else [], key="pro")
        for i, (o, s) in enumerate(rest):
            S.dma("sp", o, s, w=["posi", "invf", "dmask", "vdec", "epsr"] if i == len(rest) - 1 else [], key="pro2")
        S.op("pool", lambda e: e.memset(NHALF[:], -0.5), w=["nhalf"])
        S.op("dve", lambda e: e.tensor_copy(out=IDB[:], in_=IDF[:]), r=["idf"], w=["idb"])
        S.op("pe", lambda e: e.transpose(out=psf(1, 124), in_=VROWS[0:124, :], identity=IDF[0:124, 0:124]), r=["vrows", "idf"], w=psk(1))
        S.op("dve", lambda e: e.tensor_copy(out=VT[:], in_=psf(1, 124)), r=psk(1), w=["vt"])
        S.op("act", lambda e: e.activation(out=SCT[:], in_=VT[:, 0:32], func=AF.Silu), r=["vt"], w=["sct"])
        S.op("dve", lambda e: e.tensor_copy(out=POSF[:], in_=POSI[:]), r=["posi"], w=["posf"])
        S.op("pe", lambda e: e.transpose(out=psf(2, 64), in_=POSF[:], identity=IDF[0:64, 0:64]), r=["posf", "idf"], w=psk(2))
        S.op("dve", lambda e: e.tensor_copy(out=POST[:], in_=psf(2, 64)), r=psk(2), w=["post"])
        FTF = FT[:].bitcast(F32)
        STG = [FTF[:, i * 4096:(i + 1) * 4096].rearrange("p (k n) -> p k n", k=8) for i in range(2)]
        stgk = [[("fT", oc) for oc in range(i * 16, i * 16 + 16)] for i in range(2)]
        wada_v = wada_d.rearrange("(k p) n -> p k n", p=128)
        sct_v = SCT[:].rearrange("p (b k) -> p k b", k=8)
        modt3 = MODT[:].rearrange("p (c b) -> p c b", b=4)
        def bc8(lo):
            return VT[:, lo:lo + 8].unsqueeze(2).broadcast_to([128, 8, 4])

        def g3(t):
            return t[:].rearrange("p (k b) -> p k b", b=4)

        def adaln_stage(cg_lo, cg_hi):
            for cg in range(cg_lo, cg_hi):
                si = cg % 2
                if cg >= 2:
                    S.dma("sp", STG[si], wada_v[:, :, cg * 512:(cg + 1) * 512], w=stgk[si], key=("stg", si))
                bkm = 2 + (cg % 2)
                for k in range(8):
                    S.op("pe", (lambda e, k=k, si=si, bkm=bkm: e.matmul(PS[0:4, bkm * 512:(bkm + 1) * 512], lhsT=sct_v[:, k, :], rhs=STG[si][:, k, :],
                                                                          start=(k == 0), stop=(k == 7))), r=stgk[si] + ["sct"], w=psk(bkm))
                stq = SCR[0:4, (cg % 2) * 512:(cg % 2 + 1) * 512]
                S.op("dve", lambda e, stq=stq, bkm=bkm: e.tensor_copy(out=stq, in_=PS[0:4, bkm * 512:(bkm + 1) * 512]), r=psk(bkm), w=[("scr", cg % 2)])
                for ci in range(4):
                    cc = cg * 4 + ci
                    S.op("pe", (lambda e, cc=cc, ci=ci, stq=stq: e.transpose(out=PS[:, cc * 4:(cc + 1) * 4], in_=stq[:, ci * 128:(ci + 1) * 128], identity=IDF[0:4, 0:4])),
                         r=[("scr", cg % 2), "idf"], w=psk(0))
            if cg_hi < 12:
                return
            S.op("dve", lambda e: e.tensor_tensor(out=modt3, in0=PS[:, 0:192].rearrange("p (c b) -> p c b", b=4),
                                                  in1=VT[:, 32:80].unsqueeze(2).broadcast_to([128, 48, 4]), op=ALU.add),
                 r=psk(0) + ["vt"], w=["modt"])

            S.op("dve", lambda e: e.scalar_tensor_tensor(out=g3(GST[0]), in0=modt3[:, 8:16, :], scalar=1.0, in1=bc8(80), op0=ALU.add, op1=ALU.mult), r=["modt", "vt"], w=["gst0"])
            S.op("dve", lambda e: e.scalar_tensor_tensor(out=g3(GST[1]), in0=modt3[:, 32:40, :], scalar=1.0, in1=bc8(96), op0=ALU.add, op1=ALU.mult), r=["modt", "vt"], w=["gst1"])
            S.op("dve", lambda e: e.tensor_tensor(out=g3(GGT[0]), in0=modt3[:, 16:24, :], in1=bc8(88), op=ALU.mult), r=["modt", "vt"], w=["ggt0"])
            S.op("dve", lambda e: e.tensor_tensor(out=g3(GGT[1]), in0=modt3[:, 40:48, :], in1=bc8(104), op=ALU.mult), r=["modt", "vt"], w=["ggt1"])
        SHT = [modt3[:, 0:8, :], modt3[:, 24:32, :]]
        for cg in range(2):
            S.dma("sp", STG[cg], wada_v[:, :, cg * 512:(cg + 1) * 512], w=stgk[cg], key=("stg", cg))

        wg_ctr = [0]
        win_v = winb.rearrange("(k p) n -> p k n", p=128)
        wout_v = woutb.rearrange("(k p) n -> p k n", p=128)
        w1_v = w1b.rearrange("(k p) n -> p k n", p=128)
        w2_v = w2b.rearrange("(k p) n -> p k n", p=128)

        win_f = win_d.rearrange("(k p) n -> p k n", p=128)
        wout_f = wout_d.rearrange("(k p) n -> p k n", p=128)
        w1_f = w1_d.rearrange("(k p) n -> p k n", p=128)
        w2_f = w2_d.rearrange("(k p) n -> p k n", p=128)
        wl_ctr = [0]

        def wload(src, kdim, wbkey, srcf=None):
            n = wg_ctr[0]
            s = n % NWSLOT
            wg_ctr[0] += 1
            gi = n % 25
            dst = WS[s][:].rearrange("p (k n) -> p k n", k=kdim)
            if SLOTCAST:
                if n < 25:
                    S.dma("pool", dst, srcf, w=[("w", s)], key=("w", s))
                    S.dma("sp", src, dst, r=[("w", s)], w=[("wbg", gi)], key=("wst", gi))
                else:
                    S.dma("sp", dst, src, r=[("wbg", gi)], w=[("w", s)], key=("w", s))
            else:
                S.dma("sp", dst, src, r=[("wb", wbkey)], w=[("w", s)], key=("w", s))
            return dst, ("w", s)

        def tile_weight_plan():
            plan = []
            for g in (0, 2, 1, 3, 4, 5, 6):
                plan.append((win_v[:, :, g * 512:(g + 1) * 512], 8, "win", win_f[:, :, g * 512:(g + 1) * 512]))
            plan.append((wout_v[:, 0:4, :], 4, "wout", wout_f[:, 0:4, :]))
            plan.append((wout_v[:, 4:8, :], 4, "wout", wout_f[:, 4:8, :]))
            for g in range(8):
                plan.append((w1_v[:, :, g * 512:(g + 1) * 512], 8, "w1", w1_f[:, :, g * 512:(g + 1) * 512]))
            for g in range(8):
                plan.append((w2_v[:, g * 4:(g + 1) * 4, :], 4, "w2", w2_f[:, g * 4:(g + 1) * 4, :]))
            return plan

        pending = []

        def request_weights(t):
            pending.extend(tile_weight_plan())

        issued = []

        def next_weight():
            while pending and len(issued) < NWSLOT - 1:
                src, kdim, key, srcf = pending.pop(0)
                issued.append(wload(src, kdim, key, srcf))
            return issued.pop(0)

        x_rows = lambda t, j: slice((t * 4 + j) * 128, (t * 4 + j + 1) * 128)

        def xslot(t, j):
            return (t * 4 + j) % XSLOTS

        def xk(s):
            return [("x", s, 0), ("x", s, 1)]

        def load_x(t):
            for j in range(4):
                s = xslot(t, j)
                S.dma("sp", XS[s][:], x_d[x_rows(t, j), :], w=xk(s), key=("xld", s))

        def rstd_ops(ms_ap, ms_key, eps_ap=None):
            st, k = new_stat()
            n = ms_ap.shape[1]
            if eps_ap is None:
                S.op("pool", lambda e: e.tensor_scalar(out=st[:, 0:n], in0=ms_ap, scalar1=EPS, scalar2=None, op0=ALU.add),
                     r=[ms_key], w=[k])
            else:
                S.op("pool", lambda e: e.tensor_tensor(out=st[:, 0:n], in0=ms_ap, in1=eps_ap, op=ALU.add),
                     r=[ms_key, "epsr"], w=[k])
            S.op("pool", lambda e: e.tensor_tensor(out=st[:, 4:4 + n], in0=st[:, 0:n], in1=NHALF[:, 0:n], op=ALU.pow), r=[k, "nhalf"], w=[k])
            return st[:, 4:4 + n], k

        htv = HT[:].rearrange("p (k n) -> p k n", k=8)
        ytv = YT[:].rearrange("p (k n) -> p k n", k=8)
        ftv = FT[:].rearrange("p (k n) -> p k n", k=32)
        xyv = XY[:].rearrange("p (c n) -> p c n", c=4)
        uv = U[:].rearrange("p (c n) -> p c n", c=4)
        hk = lambda kc: [("hT", kc, jj) for jj in range(4)]
        bank_ctr = [0]

        def seq_setup(t):
            b = t // 4
            for i in range(2):
                base = 4 + 2 * i
                gt = g3(GGT[i])
                for kc in range(8):
                    S.op("pe", lambda e, kc=kc, base=base, gt=gt: e.matmul(PS[:, base * 512 + kc * 128: base * 512 + (kc + 1) * 128],
                                                                          lhsT=gt[:, kc, b:b + 1].broadcast_to([128, 128]), rhs=IDF[:], start=True, stop=True),
                         r=["ggt%d" % i, "idf"], w=psk(base, base + 1))
                S.op("act" if i == 0 else "dve",
                     (lambda e, base=base, i=i: e.activation(out=GG[i][:], in_=PS[:, base * 512: base * 512 + 1024], func=AF.Copy)) if i == 0 else
                     (lambda e, base=base, i=i: e.tensor_copy(out=GG[i][:], in_=PS[:, base * 512: base * 512 + 1024])),
                     r=psk(base, base + 1), w=[("gg", i)])

        def rot_tables(t):
            b = t // 4
            j0 = (t % 4) * 4
            cs = CS[0]
            csk = ("cs", 0)
            C1 = 6.28125
            C2 = 2 * PI - C1
            MW = SCR[:, 1536:2048]
            v3 = lambda tl: tl[:].rearrange("p (j f) -> p j f", j=4)
            S.op("dve", lambda e: e.tensor_tensor(out=v3(TT0), in0=POST[:, b * 16 + j0: b * 16 + j0 + 4].unsqueeze(2).broadcast_to([128, 4, 64]),
                                                  in1=INVF[:].unsqueeze(1).broadcast_to([128, 4, 64]), op=ALU.mult), r=["post", "invf"], w=["tt0"])
            S.op("dve", lambda e: e.tensor_scalar(out=TT1[:], in0=TT0[:], scalar1=1.0 / (2 * PI), scalar2=None, op0=ALU.mult), r=["tt0"], w=["tt1"])
            S.op("dve", lambda e: e.tensor_copy(out=TTI[:], in_=TT1[:]), r=["tt1"], w=["tti"])
            S.op("dve", lambda e: e.tensor_copy(out=TT3[:], in_=TTI[:]), r=["tti"], w=["tt3"])
            S.op("dve", lambda e: e.scalar_tensor_tensor(out=TT1[:], in0=TT3[:], scalar=-C1, in1=TT0[:], op0=ALU.mult, op1=ALU.add), r=["tt3", "tt0"], w=["tt1"])
            S.op("dve", lambda e: e.scalar_tensor_tensor(out=TT0[:], in0=TT3[:], scalar=-C2, in1=TT1[:], op0=ALU.mult, op1=ALU.add), r=["tt3", "tt1"], w=["tt0"])
            S.op("dve", lambda e: e.tensor_scalar(out=US[:, 0:256], in0=TT0[:], scalar1=0.5 * PI, scalar2=None, op0=ALU.add), r=["tt0"], w=["us0"])
            S.op("dve", lambda e: e.tensor_copy(out=US[:, 256:512], in_=TT0[:]), r=["tt0"], w=["us1"])
            S.op("dve", lambda e: e.tensor_scalar(out=MW, in0=US[:], scalar1=PI, scalar2=-2 * PI, op0=ALU.is_gt, op1=ALU.mult), r=["us0", "us1"], w=[("scr", 3)])
            S.op("dve", lambda e: e.tensor_tensor(out=US[:], in0=US[:], in1=MW, op=ALU.add), r=["us0", "us1", ("scr", 3)], w=["us0", "us1"])
            S.op("dve", lambda e: e.tensor_scalar(out=MW, in0=US[:], scalar1=-PI, scalar2=2 * PI, op0=ALU.is_lt, op1=ALU.mult), r=["us0", "us1"], w=[("scr", 3)])
            S.op("dve", lambda e: e.tensor_tensor(out=US[:], in0=US[:], in1=MW, op=ALU.add), r=["us0", "us1", ("scr", 3)], w=["us0", "us1"])
            S.op("act", lambda e: e.activation(out=cs[:, 0:256], in_=US[:, 0:256], func=AF.Sin), r=["us0"], w=[csk])
            S.op("act", lambda e: e.activation(out=cs[:, 256:512], in_=US[:, 256:512], func=AF.Sin, scale=-1.0), r=["us1"], w=[csk])
            S.op("act", lambda e: e.activation(out=cs[:, 512:768], in_=US[:, 256:512], func=AF.Sin), r=["us1"], w=[csk])

        def conv_stage(t):
            last_in_seq = (t % 4 == 3)
            bank_ctr[0] = 0
            wv, wk = next_weight()
            for c in range(4):
                bk = bank_ctr[0] % 4; bank_ctr[0] += 1
                for kc in range(8):
                    S.op("pe", lambda e, c=c, kc=kc, bk=bk, wv=wv: e.matmul(psf(bk), lhsT=wv[:, kc, c * 128:(c + 1) * 128], rhs=htv[:, kc, :], start=(kc == 0), stop=(kc == 7)),
                         r=[wk] + hk(kc), w=psk(bk))
                S.op("act", lambda e, c=c, bk=bk: e.activation(out=xyv[:, c, :], in_=psf(bk), func=AF.Copy), r=psk(bk), w=[("xy", c)])
            wv, wk = next_weight()
            for c in range(4):
                bk = bank_ctr[0] % 4; bank_ctr[0] += 1
                for kc in range(8):
                    S.op("pe", lambda e, c=c, kc=kc, bk=bk, wv=wv: e.matmul(psf(bk), lhsT=wv[:, kc, c * 128:(c + 1) * 128], rhs=htv[:, kc, :], start=(kc == 0), stop=(kc == 7)),
                         r=[wk] + hk(kc), w=psk(bk))
                S.op("dve", lambda e, c=c, bk=bk: e.tensor_tensor(out=uv[:, c, 2:514], in0=psf(bk), in1=xyv[:, c, :], op=ALU.mult), r=psk(bk) + [("xy", c)], w=[("u", c)])
                w0 = VT[:, 112 + c: 113 + c]
                w1 = VT[:, 116 + c: 117 + c]
                w2 = VT[:, 120 + c: 121 + c]
                S.op("dve", lambda e, c=c, w2=w2: e.tensor_scalar(out=xyv[:, c, :], in0=uv[:, c, 2:514], scalar1=w2, scalar2=None, op0=ALU.mult), r=[("u", c), "vt"], w=[("xy", c)])
                S.op("dve", lambda e, c=c, w1=w1: e.scalar_tensor_tensor(out=xyv[:, c, :], in0=uv[:, c, 1:513], scalar=w1, in1=xyv[:, c, :], op0=ALU.mult, op1=ALU.add), r=[("u", c), "vt", ("xy", c)], w=[("xy", c)])
                S.op("dve", lambda e, c=c, w0=w0: e.scalar_tensor_tensor(out=xyv[:, c, :], in0=uv[:, c, 0:512], scalar=w0, in1=xyv[:, c, :], op0=ALU.mult, op1=ALU.add), r=[("u", c), "vt", ("xy", c)], w=[("xy", c)])
                if not last_in_seq:
                    S.op("pool", lambda e, c=c: e.tensor_copy(out=uv[:, c, 0:2], in_=uv[:, c, 512:514]), r=[("u", c)], w=[("u", c)])
            wv, wk = next_weight()
            for c in range(4):
                bk = bank_ctr[0] % 4; bank_ctr[0] += 1
                for kc in range(8):
                    S.op("pe", lambda e, c=c, kc=kc, bk=bk, wv=wv: e.matmul(psf(bk), lhsT=wv[:, kc, c * 128:(c + 1) * 128], rhs=htv[:, kc, :], start=(kc == 0), stop=(kc == 7)),
                         r=[wk] + hk(kc), w=psk(bk))
                S.op("dve", lambda e, c=c, bk=bk: e.tensor_tensor(out=ytv[:, c, :], in0=psf(bk), in1=xyv[:, c, :], op=ALU.mult), r=psk(bk) + [("xy", c)], w=[("yTc", c)])

        def qkv_stage(t):
            cs = CS[0]
            csk = ("cs", 0)
            cs4 = cs[:].rearrange("p (a j f) -> p a j f", a=3, j=4)
            for gi in range(2):
                wv, wk = next_weight()
                for j in range(4):
                    bk = bank_ctr[0] % 8; bank_ctr[0] += 1
                    for kc in range(8):
                        S.op("pe", lambda e, j=j, kc=kc, bk=bk, wv=wv: e.matmul(psf(bk), lhsT=htv[:, kc, j * 128:(j + 1) * 128], rhs=wv[:, kc, :], start=(kc == 0), stop=(kc == 7)),
                             r=[wk, ("hT", kc, j)], w=psk(bk))
                    p4 = psf(bk).rearrange("p (h t f) -> p h t f", h=4, t=2)
                    if gi in (0, 1):
                        dst = (QR if gi == 0 else KR)[:, j * 512:(j + 1) * 512]
                        dkey = ("qr" if gi == 0 else "kr", j)
                        ab = ((gi * 4 + j) % 2) * 2
                        A = SCR[:, ab * 512:(ab + 1) * 512]
                        B = SCR[:, (ab + 1) * 512:(ab + 2) * 512]
                        A4 = A.rearrange("p (h t f) -> p h t f", h=4, t=2)
                        B4 = B.rearrange("p (h t f) -> p h t f", h=4, t=2)
                        cosb = cs4[:, 0, j, :].unsqueeze(1).unsqueeze(1).broadcast_to([128, 4, 2, 64])
                        nsinb = cs4[:, 1, j, :].unsqueeze(1).broadcast_to([128, 4, 64])
                        sinb = cs4[:, 2, j, :].unsqueeze(1).broadcast_to([128, 4, 64])
                        S.op("dve", lambda e, p4=p4, A4=A4, cosb=cosb: e.tensor_tensor(out=A4, in0=p4, in1=cosb, op=ALU.mult), r=psk(bk) + [csk], w=[("scr", ab)])
                        S.op("dve", lambda e, p4=p4, B4=B4, nsinb=nsinb: e.tensor_tensor(out=B4[:, :, 0, :], in0=p4[:, :, 1, :], in1=nsinb, op=ALU.mult), r=psk(bk) + [csk], w=[("scr", ab + 1)])
                        S.op("dve", lambda e, p4=p4, B4=B4, sinb=sinb: e.tensor_tensor(out=B4[:, :, 1, :], in0=p4[:, :, 0, :], in1=sinb, op=ALU.mult), r=psk(bk) + [csk], w=[("scr", ab + 1)])
                        S.op("pool", lambda e, A=A, B=B, dst=dst: e.tensor_tensor(out=dst, in0=A, in1=B, op=ALU.add), r=[("scr", ab), ("scr", ab + 1)], w=[dkey])

        B_T = (0, 1)
        B_O = (2, 3)
        B_SC = 4
        B_KV = 5
        B_MIX = ((6, 7), (4, 5))
        tb_ctr = [0]

        def next_tbank():
            b = B_T[tb_ctr[0] % 2]
            tb_ctr[0] += 1
            return b

        def v_blk(t, j, wv, wk):
            bk = 6 + (j % 2)
            for kc in range(8):
                S.op("pe", lambda e, j=j, kc=kc, bk=bk, wv=wv: e.matmul(psf(bk), lhsT=htv[:, kc, j * 128:(j + 1) * 128], rhs=wv[:, kc, :], start=(kc == 0), stop=(kc == 7)),
                     r=[wk, ("hT", kc, j)], w=psk(bk))
            S.op("dve", lambda e, bk=bk, j=j: e.tensor_tensor(out=VD[:, j * 512:(j + 1) * 512].rearrange("p (h e) -> p h e", h=4),
                                                              in0=psf(bk).rearrange("p (h e) -> p h e", h=4),
                                                              in1=VDEC[:].unsqueeze(2).broadcast_to([128, 4, 128]), op=ALU.mult),
                 r=psk(bk) + ["vdec"], w=[("vd", j)])

        def g_blk(t, j, wv, wk):
            bk = 6 + (j % 2)
            for kc in range(8):
                S.op("pe", lambda e, j=j, kc=kc, bk=bk, wv=wv: e.matmul(psf(bk), lhsT=htv[:, kc, j * 128:(j + 1) * 128], rhs=wv[:, kc, :], start=(kc == 0), stop=(kc == 7)),
                     r=[wk, ("hT", kc, j)], w=psk(bk))
            S.op("act", lambda e, bk=bk, j=j: e.activation(out=SG[:, j * 512:(j + 1) * 512], in_=psf(bk), func=AF.Silu), r=psk(bk), w=[("sg", j)])

        def ret_AT(t, j):
            jg = t * 4 + j
            first = (jg % 16 == 0)
            bTP = next_tbank()
            qkt = QKT[j % 2]
            qk = ("qkt", j % 2)
            tpb = PSB[:, bTP * 1024:(bTP + 1) * 1024]
            for h in range(4):
                S.op("pe", lambda e, h=h, j=j, tpb=tpb: e.transpose(out=tpb[:, h * 128:(h + 1) * 128], in_=QR[:, j * 512 + h * 128: j * 512 + (h + 1) * 128], identity=IDB[:]),
                     r=[("qr", j), "idb"], w=psk(bTP))
            for h in range(4):
                S.op("pe", lambda e, h=h, j=j, tpb=tpb: e.transpose(out=tpb[:, (4 + h) * 128:(5 + h) * 128], in_=KR[:, j * 512 + h * 128: j * 512 + (h + 1) * 128], identity=IDB[:]),
                     r=[("kr", j), "idb"], w=psk(bTP))
            S.op("act", lambda e, qkt=qkt, tpb=tpb: e.activation(out=qkt[:], in_=tpb, func=AF.Copy), r=psk(bTP), w=[qk])

        def ret_AK(t, j):
            jg = t * 4 + j
            first = (jg % 16 == 0)
            bKV = B_KV
            if jg % 16 != 15:
                for h in range(4):
                    hs = slice(h * 128, (h + 1) * 128)
                    S.op("pe", lambda e, hs=hs, j=j, bKV=bKV: e.matmul(PS[:, bKV * 512 + hs.start: bKV * 512 + hs.stop], lhsT=KR[:, j * 512 + hs.start: j * 512 + hs.stop],
                                                                      rhs=VD[:, j * 512 + hs.start: j * 512 + hs.stop], start=True, stop=True),
                         r=[("kr", j), ("vd", j)], w=psk(bKV))
                if first:
                    S.op("dve", lambda e, bKV=bKV: e.tensor_copy(out=STATE[:], in_=psf(bKV)), r=psk(bKV), w=[("state", hh) for hh in range(4)])
                else:
                    for h in range(4):
                        hs = slice(h * 128, (h + 1) * 128)
                        S.op("dve", lambda e, hs=hs, h=h, bKV=bKV: e.scalar_tensor_tensor(out=STATE[:, hs], in0=STATE[:, hs], scalar=float(GAMMA[h] ** 128),
                                                                                         in1=PS[:, bKV * 512 + hs.start: bKV * 512 + hs.stop], op0=ALU.mult, op1=ALU.add),
                             r=psk(bKV) + [("state", h)], w=[("state", h)])
                nsb = SBF[(jg + 1) % 2]
                S.op("dve", lambda e, nsb=nsb: e.tensor_copy(out=nsb[:], in_=STATE[:]), r=[("state", hh) for hh in range(4)], w=[("sbf", (jg + 1) % 2)])

        def ret_B(t, j):
            bSC = B_SC
            qkt = QKT[j % 2]
            qk = ("qkt", j % 2)
            for h in range(4):
                S.op("pe", lambda e, h=h, qkt=qkt, bSC=bSC: e.matmul(PS[:, bSC * 512 + h * 128: bSC * 512 + (h + 1) * 128], lhsT=qkt[:, (4 + h) * 128:(5 + h) * 128],
                                                                    rhs=qkt[:, h * 128:(h + 1) * 128], start=True, stop=True), r=[qk], w=psk(bSC))
            st = ST[j % 2]
            S.op("dve", lambda e, st=st, bSC=bSC: e.tensor_tensor(out=st[:], in0=psf(bSC), in1=DMASK[:], op=ALU.mult), r=psk(bSC) + ["dmask"], w=[("st", j % 2)])

        def ret_C(t, j):
            jg = t * 4 + j
            first = (jg % 16 == 0)
            bO = B_O[j % 2]
            qkt = QKT[j % 2]
            qk = ("qkt", j % 2)
            st = ST[j % 2]
            sk = ("st", j % 2)
            sbf = SBF[jg % 2]
            sbk = ("sbf", jg % 2)
            for h in range(4):
                hs = slice(h * 128, (h + 1) * 128)
                S.op("pe", lambda e, hs=hs, st=st, j=j, bO=bO, first=first: e.matmul(PS[:, bO * 512 + hs.start: bO * 512 + hs.stop], lhsT=st[:, hs],
                                                                                   rhs=VD[:, j * 512 + hs.start: j * 512 + hs.stop], start=True, stop=first),
                     r=[sk, ("vd", j)], w=psk(bO))
                if not first:
                    S.op("pe", lambda e, hs=hs, qkt=qkt, sbf=sbf, bO=bO: e.matmul(PS[:, bO * 512 + hs.start: bO * 512 + hs.stop], lhsT=qkt[:, hs], rhs=sbf[:, hs],
                                                                                 start=False, stop=True), r=[qk, sbk], w=psk(bO))
            sto, ko = new_stat()
            for h in range(4):
                hs = slice(h * 128, (h + 1) * 128)
                S.op("act", lambda e, hs=hs, h=h, sto=sto, bO=bO: e.activation(out=JUNK[:, hs], in_=PS[:, bO * 512 + hs.start: bO * 512 + hs.stop], func=AF.Square,
                                                                              scale=float(128.0 ** -0.5), accum_out=sto[:, h:h + 1]), r=psk(bO), w=[ko])
            rs, rk = rstd_ops(sto[:, 0:4], ko, eps_ap=EPSR[:])
            yr = YR[j % 2]
            for h in range(4):
                hs = slice(h * 128, (h + 1) * 128)
                S.op("dve", lambda e, hs=hs, h=h, bO=bO, rs=rs, yr=yr, j=j: e.scalar_tensor_tensor(out=yr[:, hs], in0=PS[:, bO * 512 + hs.start: bO * 512 + hs.stop],
                                                                                                scalar=rs[:, h:h + 1], in1=SG[:, j * 512 + hs.start: j * 512 + hs.stop],
                                                                                                op0=ALU.mult, op1=ALU.mult),
                     r=psk(bO) + [rk, ("sg", j)], w=[("yr", j % 2)])

        def ret_D(t, j):
            bTP = next_tbank()
            tpb = PSB[:, bTP * 1024:(bTP + 1) * 1024]
            yr = YR[j % 2]
            yk = ("yr", j % 2)
            for h in range(4):
                S.op("pe", lambda e, h=h, yr=yr, tpb=tpb: e.transpose(out=tpb[:, h * 128:(h + 1) * 128], in_=yr[:, h * 128:(h + 1) * 128], identity=IDB[:]),
                     r=[yk, "idb"], w=psk(bTP))
            S.op("act", lambda e, j=j, tpb=tpb: e.activation(out=ytv[:, 4:8, j * 128:(j + 1) * 128], in_=tpb[:, 0:512].rearrange("p (h n) -> p h n", h=4), func=AF.Copy),
                 r=psk(bTP), w=[("yTr", j)])

        pn_store = {}

        def postnorm_a(t, j, b0, ggi, half):
            mix = PS[:, b0 * 512: b0 * 512 + 1024]
            st, k = new_stat()
            S.op("act", lambda e, st=st: e.activation(out=JUNK[:], in_=mix, func=AF.Square, scale=1.0 / 32.0, accum_out=st[:, 0:1]), r=psk(b0, b0 + 1), w=[k])
            rs, rk = rstd_ops(st[:, 0:1], k)
            TMP = SCR[:, half * 1024:(half + 1) * 1024]
            tk = [("scr", 2 * half), ("scr", 2 * half + 1)]
            S.op("dve", lambda e, TMP=TMP: e.tensor_tensor(out=TMP, in0=mix, in1=GG[ggi][:], op=ALU.mult), r=psk(b0, b0 + 1) + [("gg", ggi), k], w=tk)
            pn_store[(t, j, ggi)] = (rs, rk, TMP, tk)

        def postnorm_b(t, j, ggi, store):
            s = xslot(t, j)
            X = XS[s]
            rs, rk, TMP, tk = pn_store.pop((t, j, ggi))
            S.op("dve", lambda e, X=X, TMP=TMP, rs=rs: e.scalar_tensor_tensor(out=X[:], in0=TMP, scalar=rs, in1=X[:], op0=ALU.mult, op1=ALU.add),
                 r=tk + [rk] + xk(s), w=xk(s))
            if store:
                S.dma("sp", out_d[x_rows(t, j), :], X[:], r=xk(s), w=[], key=("out", s))

        def out_blk(t, j, wa, wak, wb, wbk):
            b0 = B_MIX[j % 2][0]
            for half in range(2):
                for kc in range(8):
                    wv, wk = (wa, wak) if kc < 4 else (wb, wbk)
                    lk = ("yTc", kc) if kc < 4 else ("yTr", j)
                    S.op("pe", lambda e, j=j, kc=kc, half=half, wv=wv, b0=b0: e.matmul(psf(b0 + half), lhsT=ytv[:, kc, j * 128:(j + 1) * 128],
                                                                                     rhs=wv[:, kc % 4, half * 512:(half + 1) * 512], start=(kc == 0), stop=(kc == 7)),
                         r=[wk, lk], w=psk(b0 + half))
            postnorm_a(t, j, b0, 0, j % 2)

        rs_store = {}

        def pre_ssq(t, sub, j):
            s = xslot(t, j)
            X = XS[s]
            st, k = new_stat()
            S.op("act", lambda e, X=X, st=st: e.activation(out=JUNK[:], in_=X[:], func=AF.Square, scale=1.0 / 32.0, accum_out=st[:, 0:1]), r=xk(s), w=[k])
            rs_store[(t, sub, j)] = rstd_ops(st[:, 0:1], k)

        def pre_xn(t, sub, j):
            s = xslot(t, j)
            X = XS[s]
            rs, rk = rs_store.pop((t, sub, j))
            xn = XN[j]
            if sub == 1 and os.environ.get("K_XNPOOL", "0") == "1":
                S.op("pool", lambda e, X=X, xn=xn, rs=rs: e.tensor_scalar(out=xn[:], in0=X[:], scalar1=rs, scalar2=None, op0=ALU.mult), r=xk(s) + [rk], w=[("xn", j)])
            else:
                S.op("act", lambda e, X=X, xn=xn, rs=rs: e.activation(out=xn[:], in_=X[:], func=AF.Copy, scale=rs), r=xk(s) + [rk], w=[("xn", j)])

        def pre_act(t, sub, j):
            pre_ssq(t, sub, j)
            pre_xn(t, sub, j)

        DMAT = os.environ.get("K_DMAT", "0") == "1"

        def pre_dmat(t, sub, j):
            b = t // 4
            gs = g3(GST[sub])
            sh = SHT[sub]
            xn = XN[j]
            keys = [("hT", kc, j) for kc in range(8)]
            for kc in range(8):
                S.dma_t("sp", htv[:, kc, j * 128:(j + 1) * 128], xn[:, kc * 128:(kc + 1) * 128], r=[("xn", j)],
                        w=keys if kc in (0, 7) else [], key=("tpd", j))
            for kc in range(8):
                dst = htv[:, kc, j * 128:(j + 1) * 128]
                if kc % 2 == 0:
                    S.op("act", lambda e, kc=kc, dst=dst: e.activation(out=dst, in_=dst, func=AF.Identity, bias=sh[:, kc, b:b + 1], scale=gs[:, kc, b:b + 1]),
                         r=[("hT", kc, j), "gst%d" % sub, "modt"], w=[("hT", kc, j)])
                else:
                    S.op("dve", lambda e, kc=kc, dst=dst: e.tensor_scalar(out=dst, in0=dst, scalar1=gs[:, kc, b:b + 1], scalar2=sh[:, kc, b:b + 1], op0=ALU.mult, op1=ALU.add),
                         r=[("hT", kc, j), "gst%d" % sub, "modt"], w=[("hT", kc, j)])

        def pre_tp_big(t, sub, tpbase):
            tp = PSB[:, tpbase * 1024: tpbase * 1024 + 4096].rearrange("p (k n) -> p k n", k=8)
            tpk = psk(tpbase, tpbase + 1, tpbase + 2, tpbase + 3)
            for j in range(4):
                xn = XN[j]
                for kc in range(8):
                    S.op("pe", lambda e, xn=xn, kc=kc, j=j: e.transpose(out=tp[:, kc, j * 128:(j + 1) * 128], in_=xn[:, kc * 128:(kc + 1) * 128], identity=IDB[:]),
                         r=[("xn", j), "idb"], w=[("ps", tpbase + kc // 2)])

        def pre_evac_big(t, sub, tpbase):
            b = t // 4
            tp = PSB[:, tpbase * 1024: tpbase * 1024 + 4096].rearrange("p (k n) -> p k n", k=8)
            gs = g3(GST[sub])
            sh = SHT[sub]
            for kc in range(8):
                bkk = psk(tpbase + kc // 2)
                if (kc // 2) % 2 == 0:
                    S.op("act", lambda e, kc=kc: e.activation(out=htv[:, kc, :], in_=tp[:, kc, :], func=AF.Identity, bias=sh[:, kc, b:b + 1], scale=gs[:, kc, b:b + 1]),
                         r=bkk + ["gst%d" % sub, "modt"], w=hk(kc))
                else:
                    S.op("dve", lambda e, kc=kc: e.tensor_scalar(out=htv[:, kc, :], in0=tp[:, kc, :], scalar1=gs[:, kc, b:b + 1], scalar2=sh[:, kc, b:b + 1], op0=ALU.mult, op1=ALU.add),
                         r=bkk + ["gst%d" % sub, "modt"], w=hk(kc))

        def pre_tp_blk(t, sub, j):
            b = t // 4
            gs = g3(GST[sub])
            sh = SHT[sub]
            xn = XN[j]
            for half in range(2):
                bTP = B_T[half]
                tpb = PSB[:, bTP * 1024:(bTP + 1) * 1024]
                for kq in range(4):
                    kc = half * 4 + kq
                    S.op("pe", lambda e, xn=xn, kc=kc, kq=kq, tpb=tpb: e.transpose(out=tpb[:, kq * 128:(kq + 1) * 128], in_=xn[:, kc * 128:(kc + 1) * 128], identity=IDB[:]),
                         r=[("xn", j), "idb"], w=psk(bTP))
            for kq in range(4):
                kc = kq
                tpb = PSB[:, B_T[0] * 1024:(B_T[0] + 1) * 1024]
                S.op("act", lambda e, kc=kc, kq=kq, j=j, tpb=tpb: e.activation(out=htv[:, kc, j * 128:(j + 1) * 128], in_=tpb[:, kq * 128:(kq + 1) * 128], func=AF.Identity,
                                                                              bias=sh[:, kc, b:b + 1], scale=gs[:, kc, b:b + 1]),
                     r=psk(B_T[0]) + ["gst%d" % sub, "modt"], w=[("hT", kc, j)])
                kc = 4 + kq
                tpb1 = PSB[:, B_T[1] * 1024:(B_T[1] + 1) * 1024]
                S.op("dve", lambda e, kc=kc, kq=kq, j=j, tpb1=tpb1: e.tensor_scalar(out=htv[:, kc, j * 128:(j + 1) * 128], in0=tpb1[:, kq * 128:(kq + 1) * 128],
                                                                                   scalar1=gs[:, kc, b:b + 1], scalar2=sh[:, kc, b:b + 1], op0=ALU.mult, op1=ALU.add),
                     r=psk(B_T[1]) + ["gst%d" % sub, "modt"], w=[("hT", kc, j)])

        def fc1_evac(oc, bk):
            R = SCR[:, (oc % 4) * 512:(oc % 4 + 1) * 512]
            rk = ("scr", oc % 4)
            S.op("act", lambda e, bk=bk, R=R: e.activation(out=R, in_=psf(bk), func=AF.Relu), r=psk(bk), w=[rk])
            S.op("dve" if oc % 2 == 0 else "pool", lambda e, oc=oc, R=R: e.tensor_tensor(out=ftv[:, oc, :], in0=R, in1=R, op=ALU.mult), r=[rk], w=[("fT", oc)])

        def fc1_stage(t):
            nxt = t + 1 < NT
            wg = [next_weight(), next_weight()]
            split_banks = [2, 3, 6, 7, 4, 5]
            for oc in range(6):
                wv, wk = wg[oc // 4]
                i = oc % 4
                bk = split_banks[oc]
                for kc in range(8):
                    S.op("pe", lambda e, i=i, kc=kc, bk=bk, wv=wv: e.matmul(PS[:, bk * 512: bk * 512 + 256], lhsT=wv[:, kc, i * 128:(i + 1) * 128], rhs=htv[:, kc, 0:256],
                                                                          start=(kc == 0), stop=(kc == 7)),
                         r=[wk] + [("hT", kc, jj) for jj in range(2)], w=psk(bk))
            (pre_dmat if DMAT else pre_tp_blk)(t, 1, 2)
            (pre_dmat if DMAT else pre_tp_blk)(t, 1, 3)
            for oc in range(6):
                wv, wk = wg[oc // 4]
                i = oc % 4
                bk = split_banks[oc]
                for kc in range(8):
                    S.op("pe", lambda e, i=i, kc=kc, bk=bk, wv=wv: e.matmul(PS[:, bk * 512 + 256: bk * 512 + 512], lhsT=wv[:, kc, i * 128:(i + 1) * 128], rhs=htv[:, kc, 256:512],
                                                                          start=(kc == 0), stop=(kc == 7)),
                         r=[wk, ("hT", kc, 2), ("hT", kc, 3)], w=psk(bk))
                fc1_evac(oc, bk)
            if nxt:
                pre_act(t + 1, 0, 0)
            for oc in range(6, 32):
                g = oc // 4
                i = oc % 4
                if g >= 2 and i == 0:
                    wg.append(next_weight())
                wv, wk = wg[g]
                bk = bank_ctr[0] % 4; bank_ctr[0] += 1
                for kc in range(8):
                    S.op("pe", lambda e, i=i, kc=kc, bk=bk, wv=wv: e.matmul(psf(bk), lhsT=wv[:, kc, i * 128:(i + 1) * 128], rhs=htv[:, kc, :], start=(kc == 0), stop=(kc == 7)),
                         r=[wk] + hk(kc), w=psk(bk))
                fc1_evac(oc, bk)
                if nxt and i == 3 and g in (1, 2, 3):
                    pre_act(t + 1, 0, g)
                if nxt and i == 3 and g == 5 and not DMAT:
                    pre_tp_big(t + 1, 0, 4)
            if nxt:
                if DMAT:
                    for j in range(4):
                        pre_dmat(t + 1, 0, j)
                else:
                    pre_evac_big(t + 1, 0, 4)

        def fc2_stage(t):
            for g in range(8):
                wv, wk = next_weight()
                for jp in (((0,), (1,), (2,), (3,)) if g == 7 else ((0, 1), (2, 3))):
                    for ki in range(4):
                        kc = g * 4 + ki
                        for j in jp:
                            for half in range(2):
                                S.op("pe", lambda e, j=j, kc=kc, ki=ki, half=half, wv=wv: e.matmul(psf(2 * j + half), lhsT=ftv[:, kc, j * 128:(j + 1) * 128],
                                                                                                 rhs=wv[:, ki, half * 512:(half + 1) * 512], start=(kc == 0), stop=(kc == 31)),
                                     r=[wk, ("fT", kc)], w=psk(2 * j + half))
            postnorm_a(t, 0, 0, 1, 0)
            postnorm_a(t, 1, 2, 1, 1)
            postnorm_b(t, 0, 1, True)
            postnorm_a(t, 2, 4, 1, 0)
            postnorm_b(t, 1, 1, True)
            postnorm_a(t, 3, 6, 1, 1)
            postnorm_b(t, 2, 1, True)
            postnorm_b(t, 3, 1, True)

        load_x(0)
        for t in range(NT):
            request_weights(t)
        rot_tables(0)
        for j in range(4):
            pre_act(0, 0, j)
        adaln_stage(0, 3)
        pre_tp_big(0, 0, 4)
        adaln_stage(3, 12)
        pre_evac_big(0, 0, 4)
        for r in range(8 if KPRO & 1 else 0):
            S.dma("pool", w1b[r * 128:(r + 1) * 128, :], w1_d[r * 128:(r + 1) * 128, :], r=["modt"], w=[("wb", "w1")] if r == 7 else [], key="cast_w1")
        for t in range(NT):
            if t % 4 == 0:
                S.op("pool", lambda e: e.memset(uv[:, :, 0:2], 0.0), w=[("u", c) for c in range(4)])
            if t + 1 < NT:
                load_x(t + 1)
            conv_stage(t)
            if t % 4 == 0:
                seq_setup(t)
            qkv_stage(t)
            if t == 0:
                for r in range(8 if KPRO & 1 else 0):
                    S.dma("pool", w2b[r * 512:(r + 1) * 512, :], w2_d[r * 512:(r + 1) * 512, :], r=["modt"], w=[("wb", "w2")] if r == 7 else [], key="cast_w2")
            vw, vwk = next_weight()
            gw, gwk = next_weight()
            ret_AT(t, 0); v_blk(t, 0, vw, vwk)
            ret_AT(t, 1); v_blk(t, 1, vw, vwk)
            ret_B(t, 0); v_blk(t, 2, vw, vwk)
            ret_B(t, 1); v_blk(t, 3, vw, vwk)
            ret_AK(t, 0); g_blk(t, 0, gw, gwk)
            ret_C(t, 0); g_blk(t, 1, gw, gwk)
            ret_AK(t, 1); ret_AT(t, 2); g_blk(t, 2, gw, gwk)
            ret_B(t, 2); ret_C(t, 1); ret_AK(t, 2); ret_AT(t, 3); g_blk(t, 3, gw, gwk)
            ret_B(t, 3); ret_D(t, 0); ret_C(t, 2); ret_AK(t, 3)
            wa, wak = next_weight()
            wb, wbk = next_weight()
            ret_D(t, 1); ret_C(t, 3)
            out_blk(t, 0, wa, wak, wb, wbk)
            ret_D(t, 2)
            postnorm_b(t, 0, 0, False)
            pre_ssq(t, 1, 0)
            out_blk(t, 1, wa, wak, wb, wbk)
            pre_xn(t, 1, 0)
            postnorm_b(t, 1, 0, False)
            pre_ssq(t, 1, 1)
            out_blk(t, 2, wa, wak, wb, wbk)
            ret_D(t, 3)
            (pre_dmat if DMAT else pre_tp_blk)(t, 1, 0)
            pre_xn(t, 1, 1)
            postnorm_b(t, 2, 0, False)
            pre_ssq(t, 1, 2)
            out_blk(t, 3, wa, wak, wb, wbk)
            (pre_dmat if DMAT else pre_tp_blk)(t, 1, 1)
            pre_xn(t, 1, 2)
            postnorm_b(t, 3, 0, False)
            pre_ssq(t, 1, 3)
            pre_xn(t, 1, 3)
            fc1_stage(t)
            if t + 1 < NT:
                rot_tables(t + 1)
            fc2_stage(t)
        S.emit(nc, es)
    return nc


def _consts():
    ident = np.eye(128, dtype=np.float32)
    invf = (np.float32(10000.0) ** (-(np.arange(64, dtype=np.float32)) / np.float32(64))).astype(np.float32)
    invf = np.ascontiguousarray(np.broadcast_to(invf[None, :], (128, 64)))
    m = np.arange(128)[:, None].astype(np.float64)
    c = np.arange(128)[None, :].astype(np.float64)
    same = (m // 64) == (c // 64)
    causal_x = (m // 64 == 0) & (c // 64 == 1)
    dmask = np.zeros((128, 4, 128), np.float64)
    vdec = np.zeros((128, 4), np.float64)
    epsr = np.zeros((128, 4), np.float64)
    for h in range(4):
        lg = np.log1p(-2.0 ** (-5 - h))
        dd = np.where(same, np.exp(lg * np.abs(c - m)), np.where(causal_x, np.exp(lg * (c - m)), 0.0))
        dmask[:, h, :] = dd * np.exp(-lg * (128.0 + c - m))
        vdec[:, h] = np.exp(lg * (127.0 - np.arange(128)))
        epsr[:, h] = EPS * 128.0 * np.exp(-2.0 * lg * (np.arange(128) + 1.0))
    return {
        "ident": ident, "invf": invf, "dmask": np.ascontiguousarray(dmask.reshape(128, 512).astype(np.float32)),
        "vdec": vdec.astype(np.float32), "epsr": epsr.astype(np.float32),
    }


def make_in_maps(inputs, n_cores=8, NT=16):
    f = lambda a: np.ascontiguousarray(np.asarray(a))
    x = f(inputs["x"]); c = f(inputs["c"]); pos = f(inputs["positions"])
    shared = {
        "w_ada": f(inputs["w_ada"])[0], "b_ada": f(inputs["b_ada"])[0].reshape(48, 128),
        "g_pre_mix": f(inputs["g_pre_mix"])[0].reshape(8, 128), "g_post_mix": f(inputs["g_post_mix"])[0].reshape(8, 128),
        "w_in": f(inputs["w_in"])[0], "conv_w": f(inputs["conv_w"])[0].reshape(12, 128), "w_out": f(inputs["w_out"])[0],
        "g_pre_mlp": f(inputs["g_pre_mlp"])[0].reshape(8, 128), "g_post_mlp": f(inputs["g_post_mlp"])[0].reshape(8, 128),
        "w_fc1": f(inputs["w_fc1"])[0], "w_fc2": f(inputs["w_fc2"])[0],
    }
    shared.update(_consts())
    maps = []
    for i in range(n_cores):
        bs = slice(i * NB_CORE, (i + 1) * NB_CORE)
        m = dict(shared)
        m["x"] = np.ascontiguousarray(x[bs].reshape(NB_CORE * SEQ, D)[: NT * 512])
        m["c"] = np.ascontiguousarray(c[bs].reshape(32, 128))
        m["pos"] = np.ascontiguousarray(pos[bs].reshape(64, 128).astype(np.int32))
        maps.append(m)
    return maps


_PROG = {}


def kernel(**inputs):
    if 16 not in _PROG:
        _PROG[16] = build_program(16)
    nc = _PROG[16]
    maps = make_in_maps(inputs, 8, 16)
    res = run_bass_kernel_spmd(nc, maps, core_ids=list(range(8)))
    outs = [np.asarray(r["out"]).reshape(NB_CORE, SEQ, D) for r in res.results]
    return np.concatenate(outs, axis=0).astype(np.float32, copy=False)
```
